# Optimizing a Trainium2 kernel written in Bass

```python
import math
import jax, jax.numpy as jnp
from jax import lax
import numpy as np

D_MODEL = 1024
BATCH = 4
SEQ = 4096
DEPTH = 2

D_MIX = D_MODEL
W_CONV = D_MIX // 4
W_GLA = D_MIX // 4
W_MLA = D_MIX // 4
W_CFM = D_MIX // 4
SC_WIDTH = 3
GLA_HEADS = 4
GLA_DV = W_GLA // GLA_HEADS
GLA_DK = GLA_DV // 2
GLA_LOWRANK = 16
GLA_TAU = 16.0
GLA_CHUNK = 64
MLA_HEADS = 4
MLA_V_DIM = W_MLA // MLA_HEADS
MLA_NOPE = 64
MLA_ROPE = 32
MLA_Q_RANK = 256
MLA_KV_RANK = 128
ROPE_BASE = 10000.0
Q_BLOCK = 128
CFM_WIDTH = 31
D_FF = 2816
FFN_CONV_WIDTH = 3
DN_ALPHA = (2.0 * DEPTH) ** 0.25
DN_BETA = (8.0 * DEPTH) ** -0.25
EPS = 1e-5

SPLIT_SIZES = (
    W_CONV, W_CONV, W_CONV,
    GLA_HEADS * GLA_DK, GLA_HEADS * GLA_DK, W_GLA, W_GLA,
    GLA_LOWRANK, GLA_LOWRANK,
    MLA_Q_RANK, MLA_KV_RANK, MLA_ROPE,
    2 * W_CFM,
)
D_IN = sum(SPLIT_SIZES)

kernel_name = 'hybrid_parallel_groups_deepnorm_encoder'


def _split_points():
    pts, acc = [], 0
    for s in SPLIT_SIZES[:-1]:
        acc += s
        pts.append(acc)
    return pts


def layer_norm(x, g, b):
    xf = x.astype(jnp.float32)
    mu = jnp.mean(xf, axis=-1, keepdims=True)
    var = jnp.mean(jnp.square(xf - mu), axis=-1, keepdims=True)
    return ((xf - mu) * lax.rsqrt(var + EPS) * g.astype(jnp.float32) + b.astype(jnp.float32)).astype(x.dtype)


def rms_norm(x, g):
    xf = x.astype(jnp.float32)
    return (xf * lax.rsqrt(jnp.mean(xf * xf, axis=-1, keepdims=True) + EPS) * g.astype(jnp.float32)).astype(x.dtype)


def depthwise_conv(x, w):
    k, c = w.shape
    return lax.conv_general_dilated(
        x, w[:, None, :].astype(x.dtype), window_strides=(1,), padding=[(k // 2, k // 2)],
        dimension_numbers=('NWC', 'WIO', 'NWC'), feature_group_count=c)


def rope_tables(positions):
    inv = ROPE_BASE ** (-jnp.arange(0, MLA_ROPE, 2, dtype=jnp.float32) / MLA_ROPE)
    ang = positions.astype(jnp.float32)[..., None] * inv
    return jnp.cos(ang), jnp.sin(ang)


def apply_rope(x, cos, sin):
    half = MLA_ROPE // 2
    xf = x.astype(jnp.float32)
    x1, x2 = xf[..., :half], xf[..., half:]
    c, s = cos[:, :, None, :], sin[:, :, None, :]
    return jnp.concatenate([x1 * c - x2 * s, x1 * s + x2 * c], axis=-1).astype(x.dtype)


def gla_scan(q, k, v, log_a, include_diag):
    bsz, nh, s, dk = q.shape
    dv = v.shape[-1]
    n = s // GLA_CHUNK

    def chunks(t):
        return t.astype(jnp.float32).reshape(bsz, nh, n, GLA_CHUNK, t.shape[-1]).transpose(2, 0, 1, 3, 4)

    idx = jnp.arange(GLA_CHUNK)
    mask = (idx[:, None] >= idx[None, :]) if include_diag else (idx[:, None] > idx[None, :])
    mask5 = mask[None, None, :, :, None]

    def step(state, inp):
        qc, kc, vc, ac = inp
        b = jnp.cumsum(ac, axis=2)
        o_inter = jnp.einsum('bhtk,bhkv->bhtv', qc * jnp.exp(b), state)
        rel = b[:, :, :, None, :] - b[:, :, None, :, :]
        decay = jnp.where(mask5, jnp.exp(jnp.where(mask5, rel, 0.0)), 0.0)
        scores = jnp.einsum('bhtk,bhsk,bhtsk->bhts', qc, kc, decay)
        o = o_inter + jnp.einsum('bhts,bhsv->bhtv', scores, vc)
        b_last = b[:, :, -1:, :]
        state = jnp.exp(b_last[:, :, 0, :])[..., None] * state + \
            jnp.einsum('bhsk,bhsv->bhkv', kc * jnp.exp(b_last - b), vc)
        return state, o

    init = jnp.zeros((bsz, nh, dk, dv), jnp.float32)
    _, o = lax.scan(step, init, (chunks(q), chunks(k), chunks(v), chunks(log_a)))
    return o.transpose(1, 2, 0, 3, 4).reshape(bsz, nh, s, dv)


def blocked_attention(q, k, v, scale):
    bsz, s, nh, d = q.shape
    nb = s // Q_BLOCK
    qb = q.reshape(bsz, nb, Q_BLOCK, nh, d).transpose(1, 0, 2, 3, 4)

    def blk(qi):
        sc = jnp.einsum('bqhd,bkhd->bhqk', qi, k).astype(jnp.float32) * scale
        p = jax.nn.softmax(sc, axis=-1)
        return jnp.einsum('bhqk,bkhd->bqhd', p.astype(v.dtype), v)

    o = lax.map(blk, qb)
    return o.transpose(1, 0, 2, 3, 4).reshape(bsz, s, nh, v.shape[-1])


def token_mixers(h, cos, sin, w_in, w_sc_conv, w_gla_a_up, b_gla_a, g_gla_head, g_mla_q, w_mla_uq,
                 g_mla_kv, w_mla_ukv, w_cfm_dw, g_cfm_ln, b_cfm_ln, g_branch, w_out):
    bsz, s, _ = h.shape
    proj = h @ w_in
    (sc_b, sc_c, sc_h, gq, gk, gv, gg, ga_f, ga_b, cq, ckv, kr, cfm_in) = jnp.split(proj, _split_points(), axis=-1)

    y_a = sc_b * depthwise_conv(sc_c * sc_h, w_sc_conv)

    def heads(t, d):
        return t.reshape(bsz, s, GLA_HEADS, d).transpose(0, 2, 1, 3)

    q = heads(gq, GLA_DK) * (GLA_DK ** -0.5)
    k = heads(gk, GLA_DK)
    v = heads(gv, GLA_DV)
    la_f = heads(jax.nn.log_sigmoid((ga_f @ w_gla_a_up[0] + b_gla_a[0]).astype(jnp.float32)) / GLA_TAU, GLA_DK)
    la_b = heads(jax.nn.log_sigmoid((ga_b @ w_gla_a_up[1] + b_gla_a[1]).astype(jnp.float32)) / GLA_TAU, GLA_DK)
    flip = lambda t: jnp.flip(t, axis=2)
    o_f = gla_scan(q, k, v, la_f, True)
    o_b = flip(gla_scan(flip(q), flip(k), flip(v), flip(la_b), False))
    o = (o_f + o_b).transpose(0, 2, 1, 3)
    o = rms_norm(o, g_gla_head.reshape(GLA_HEADS, GLA_DV)).reshape(bsz, s, W_GLA)
    y_b = o.astype(h.dtype) * jax.nn.silu(gg)

    qm = (rms_norm(cq, g_mla_q) @ w_mla_uq).reshape(bsz, s, MLA_HEADS, MLA_NOPE + MLA_ROPE)
    q_c = jnp.concatenate([qm[..., :MLA_NOPE], apply_rope(qm[..., MLA_NOPE:], cos, sin)], axis=-1)
    kvm = (rms_norm(ckv, g_mla_kv) @ w_mla_ukv).reshape(bsz, s, MLA_HEADS, MLA_NOPE + MLA_V_DIM)
    k_rope = jnp.broadcast_to(apply_rope(kr[:, :, None, :], cos, sin), (bsz, s, MLA_HEADS, MLA_ROPE))
    k_c = jnp.concatenate([kvm[..., :MLA_NOPE], k_rope], axis=-1)
    v_c = kvm[..., MLA_NOPE:]
    y_c = blocked_attention(q_c, k_c, v_c, (MLA_NOPE + MLA_ROPE) ** -0.5).reshape(bsz, s, W_MLA)

    a, gate = jnp.split(cfm_in, 2, axis=-1)
    u = depthwise_conv(a * jax.nn.sigmoid(gate), w_cfm_dw)
    y_d = jax.nn.silu(layer_norm(u, g_cfm_ln, b_cfm_ln))

    gains = jnp.split(g_branch, [W_CONV, W_CONV + W_GLA, W_CONV + W_GLA + W_MLA])
    y = jnp.concatenate([rms_norm(yb, gb) for yb, gb in zip((y_a, y_b, y_c, y_d), gains)], axis=-1)
    return y @ w_out


def conv_ffn(h, w_up, w_dw, w_down):
    u = depthwise_conv(h @ w_up, w_dw)
    g, val = jnp.split(u, 2, axis=-1)
    return (jax.nn.silu(g) * val) @ w_down


def setup_inputs(seed: int = 0) -> dict:
    key = jax.random.key(seed)
    ks = jax.random.split(key, 32)
    L = DEPTH
    nrm = lambda k, shape, scale: jax.random.normal(k, shape, jnp.float32) * scale
    gain = lambda k, shape: 1.0 + 0.02 * jax.random.normal(k, shape, jnp.float32)
    offsets = jax.random.randint(ks[1], (BATCH, 1), 0, SEQ, dtype=jnp.int32)
    positions = jnp.arange(SEQ, dtype=jnp.int32)[None, :] + offsets
    return {
        'x': jax.random.normal(ks[0], (BATCH, SEQ, D_MODEL), jnp.float32),
        'positions': positions,
        'ln_in_g': gain(ks[2], (D_MODEL,)),
        'ln_in_b': nrm(ks[3], (D_MODEL,), 0.02),
        'w_in': nrm(ks[4], (L, D_MODEL, D_IN), D_MODEL ** -0.5),
        'w_sc_conv': nrm(ks[5], (L, SC_WIDTH, W_CONV), SC_WIDTH ** -0.5),
        'w_gla_a_up': nrm(ks[6], (L, 2, GLA_LOWRANK, GLA_HEADS * GLA_DK), GLA_LOWRANK ** -0.5),
        'b_gla_a': nrm(ks[7], (L, 2, GLA_HEADS * GLA_DK), 0.02),
        'g_gla_head': gain(ks[8], (L, W_GLA)),
        'g_mla_q': gain(ks[9], (L, MLA_Q_RANK)),
        'w_mla_uq': nrm(ks[10], (L, MLA_Q_RANK, MLA_HEADS * (MLA_NOPE + MLA_ROPE)), MLA_Q_RANK ** -0.5),
        'g_mla_kv': gain(ks[11], (L, MLA_KV_RANK)),
        'w_mla_ukv': nrm(ks[12], (L, MLA_KV_RANK, MLA_HEADS * (MLA_NOPE + MLA_V_DIM)), MLA_KV_RANK ** -0.5),
        'w_cfm_dw': nrm(ks[13], (L, CFM_WIDTH, W_CFM), CFM_WIDTH ** -0.5),
        'g_cfm_ln': gain(ks[14], (L, W_CFM)),
        'b_cfm_ln': nrm(ks[15], (L, W_CFM), 0.02),
        'g_branch': gain(ks[16], (L, D_MIX)),
        'w_out': nrm(ks[17], (L, D_MIX, D_MODEL), DN_BETA * D_MIX ** -0.5),
        'ln1_g': gain(ks[18], (L, D_MODEL)),
        'ln1_b': nrm(ks[19], (L, D_MODEL), 0.02),
        'w_ffn_up': nrm(ks[20], (L, D_MODEL, 2 * D_FF), D_MODEL ** -0.5),
        'w_ffn_dw': nrm(ks[21], (L, FFN_CONV_WIDTH, 2 * D_FF), FFN_CONV_WIDTH ** -0.5),
        'w_ffn_down': nrm(ks[22], (L, D_FF, D_MODEL), DN_BETA * D_FF ** -0.5),
        'ln2_g': gain(ks[23], (L, D_MODEL)),
        'ln2_b': nrm(ks[24], (L, D_MODEL), 0.02),
    }


def reference(x, positions, ln_in_g, ln_in_b, w_in, w_sc_conv, w_gla_a_up, b_gla_a, g_gla_head, g_mla_q,
              w_mla_uq, g_mla_kv, w_mla_ukv, w_cfm_dw, g_cfm_ln, b_cfm_ln, g_branch, w_out, ln1_g, ln1_b,
              w_ffn_up, w_ffn_dw, w_ffn_down, ln2_g, ln2_b):
    cos, sin = rope_tables(positions)
    h = layer_norm(x, ln_in_g, ln_in_b)
    for l in range(DEPTH):
        mix = token_mixers(h, cos, sin, w_in[l], w_sc_conv[l], w_gla_a_up[l], b_gla_a[l], g_gla_head[l],
                           g_mla_q[l], w_mla_uq[l], g_mla_kv[l], w_mla_ukv[l], w_cfm_dw[l], g_cfm_ln[l],
                           b_cfm_ln[l], g_branch[l], w_out[l])
        h = layer_norm(DN_ALPHA * h + mix, ln1_g[l], ln1_b[l])
        h = layer_norm(DN_ALPHA * h + conv_ffn(h, w_ffn_up[l], w_ffn_dw[l], w_ffn_down[l]), ln2_g[l], ln2_b[l])
    return h
```

```python
import math
import numpy as np
import concourse.bass as bass
import concourse.mybir as mybir
from concourse.bass_utils import run_bass_kernel_spmd

F32 = mybir.dt.float32
BF16 = mybir.dt.bfloat16
I32 = mybir.dt.int32
AF = mybir.ActivationFunctionType
ALU = mybir.AluOpType

D = 1024
DFF = 2816
NEXT = 2656
G1, GA, G2, G3, G4, B1G, B2G = 0, 512, 544, 864, 1376, 1888, 2400
EPS = 1e-5
DEPTH = 2
ALPHA = (2.0 * DEPTH) ** 0.25
TW = 512
STRICT_WAR = False
import os as _os
STQ = _os.environ.get("STQ", "act")
CONV_ON = _os.environ.get("CONV_ON", "dve")

C_ID, C_TRIU, C_TRID, C_ONES = 0, 128, 256, 384
C_HM, C_ROPE, C_LNIN, C_SELW = 512, 516, 520, 536
NCONST = 544
M_MU, M_MD, M_BM, M_CME, M_CMO, M_SEL, M_J = 0, 128, 256, 512, 768, 1024, 1280
NMASK = 1408
V_GH, V_GQ, V_GKV, V_SCW, V_CFW, V_CFG, V_CFB, V_GBR, V_L1G, V_L1B, V_FDW, V_L2G, V_L2B = \
    0, 2, 4, 5, 11, 73, 75, 77, 85, 93, 101, 233, 241
NV = 256


class Sched:
    COMPUTE = ("pe", "act", "dve", "pool")

    def __init__(self, nc):
        self.nc = nc
        self.engs = {"pe": nc.tensor, "act": nc.scalar, "dve": nc.vector,
                     "pool": nc.gpsimd, "sp": nc.sync}
        self.ops = []
        self.last_w = {}
        self.readers = {}

    def op(self, eng, fn, reads=(), writes=(), dma_key=None, cc=False):
        reads = [k[0] if (isinstance(k, tuple) and isinstance(k[0], tuple)) else k for k in reads]
        writes = [k[0] if (isinstance(k, tuple) and isinstance(k[0], tuple)) else k for k in writes]
        deps = {}
        for k in reads:
            j = self.last_w.get(k)
            if j is not None:
                deps[j] = "raw"
            if isinstance(k, tuple) and k[0] == "ps":
                for j in self.readers.get(k, ()):
                    if self.ops[j]["eng"] != eng and j not in deps:
                        deps[j] = "psx"
        for k in writes:
            j = self.last_w.get(k)
            if j is not None and j not in deps:
                deps[j] = "waw"
            for j in self.readers.get(k, ()):
                if j not in deps:
                    deps[j] = "war"
        idx = len(self.ops)
        self.ops.append(dict(eng=eng, fn=fn, deps=deps, dma_key=dma_key, cc=cc))
        for k in reads:
            lst = self.readers.setdefault(k, [])
            if dma_key is None:
                lst[:] = [j for j in lst
                          if not (self.ops[j]["dma_key"] is None and self.ops[j]["eng"] == eng)]
            lst.append(idx)
        for k in writes:
            self.last_w[k] = idx
            self.readers[k] = []
        return idx

    def fence(self):
        self.ops.append(dict(eng=None, fn=None, deps={}, dma_key=None, cc=False))

    def dma(self, queue, out, in_, reads=(), writes=(), key=None, slow=False):
        eng = self.engs[queue]
        if slow:
            return self.op(queue, lambda: eng.dma_start(out=out, in_=in_, allow_slow_non_contiguous=True),
                           reads=reads, writes=writes, dma_key=key)
        return self.op(queue, lambda: eng.dma_start(out=out, in_=in_),
                       reads=reads, writes=writes, dma_key=key)

    def emit(self):
        nc, ops = self.nc, self.ops

        def skip(o, pj, kind):
            return (pj["dma_key"] is None and o["dma_key"] is None and pj["eng"] == o["eng"]
                    and (o["eng"] == "pe" or kind == "psx" or
                         (kind in ("war", "waw") and not (STRICT_WAR or o["eng"] == "pool"))))
        needed = [False] * len(ops)
        lastc = {}
        for i, o in enumerate(ops):
            if o["eng"] is None:
                for j in lastc.values():
                    needed[j] = True
                continue
            for j, kind in o["deps"].items():
                pj = ops[j]
                if pj["dma_key"] is None and not skip(o, pj, kind):
                    needed[j] = True
            if o["dma_key"] is None:
                lastc[o["eng"]] = i
        sems = {e: nc.alloc_semaphore(f"sem_{e}") for e in self.COMPUTE}
        dma_sems, dma_tot = {}, {}
        cnt = {e: 0 for e in self.COMPUTE}
        val = [None] * len(ops)
        seen = {e: {} for e in self.engs}
        pending = {e: {} for e in self.engs}
        for i, o in enumerate(ops):
            if o["eng"] is None:
                for e in self.engs:
                    for ce in self.COMPUTE:
                        if cnt[ce] > 0:
                            pending[e][("c", ce)] = cnt[ce]
                    for k, tot in dma_tot.items():
                        pending[e][("dma", k)] = tot
                continue
            e = o["eng"]
            eng = self.engs[e]
            want = dict(pending[e])
            pending[e] = {}
            for j, kind in o["deps"].items():
                pj = ops[j]
                if pj["dma_key"] is not None:
                    want[("dma", pj["dma_key"])] = dma_tot[pj["dma_key"]]
                elif not skip(o, pj, kind):
                    sk = ("c", pj["eng"])
                    want[sk] = max(want.get(sk, 0), val[j])
            for sk, v in want.items():
                if seen[e].get(sk, 0) >= v:
                    continue
                if sk == ("c", e):
                    if o["dma_key"] is None and v > cnt[e]:
                        continue
                eng.wait_ge(dma_sems[sk[1]] if sk[0] == "dma" else sems[sk[1]], v)
                seen[e][sk] = v
            ins = o["fn"]()
            if o["dma_key"] is not None:
                k = o["dma_key"]
                if k not in dma_sems:
                    dma_sems[k] = nc.alloc_semaphore(f"dsem_{len(dma_sems)}")
                    dma_tot[k] = 0
                inc = 1 if o["cc"] else 16
                dma_tot[k] += inc
                ins.then_inc(dma_sems[k], inc)
            elif needed[i]:
                cnt[e] += 1
                ins.then_inc(sems[e], 1)
                val[i] = cnt[e]
        sp = self.engs["sp"]
        for k, tot in dma_tot.items():
            sp.wait_ge(dma_sems[k], tot)
        return dict(n_ops=len(ops), n_dma_sems=len(dma_sems))


class Builder:
    def __init__(self, S, T, L, pair, debug=False):
        self.S, self.T, self.L, self.pair, self.debug = S, T, L, pair, debug
        self.NT, self.NTO = S // TW, T // TW
        nc = self.nc = bass.Bass("TRN2", target_bir_lowering=False)
        self.sc = Sched(nc)
        self._bank = 0
        self.wk = {}
        self.b2st = {}
        self.late_per_tile = 9
        self.pref = set()
        self.SCP = (0, 1, 2)
        self.GP = (3, 4, 5)
        self._ctr = {}
        self.alloc()

    def dram(self, name, shape, dt):
        kind = "ExternalOutput" if self.debug else "Internal"
        return self.nc.dram_tensor(name, list(shape), dt, kind=kind)

    def alloc(self):
        nc, S, T, L = self.nc, self.S, self.T, self.L
        di = lambda n, s, dt=F32: nc.dram_tensor(n, list(s), dt, kind="ExternalInput")
        self.x = di("x", [S, D])
        self.pos = di("pos", [1, S], I32)
        self.consts_d = di("consts", [128, NCONST])
        self.masks_d = di("masks", [128, NMASK])
        self.vecs_d = di("vecs", [L, 128, NV])
        self.w_in_d = di("w_in", [L, 128, 8, NEXT])
        self.w_uq_d = di("w_uq", [L, 128, 2, 768])
        self.w_ukv_d = di("w_ukv", [L, 128, 512])
        self.w_a_d = di("w_a", [L, 33, 256])
        self.w_out_d = di("w_out", [L, 128, 8, D])
        self.w_up_d = di("w_up", [L, 128, 8, 2 * DFF])
        self.w_dn_d = di("w_dn", [L, 128, 22, D])
        self.out = nc.dram_tensor("out", [T, D], F32, kind="ExternalOutput")
        it = lambda n, s, dt: nc.dram_tensor(n, list(s), dt, kind="Internal")
        self.w_in_s = it("w_in_s", [L, 128, 8, NEXT], BF16)
        self.w_uq_s = it("w_uq_s", [L, 128, 2, 768], BF16)
        self.w_ukv_s = it("w_ukv_s", [L, 128, 512], BF16)
        self.w_out_s = it("w_out_s", [L, 128, 8, D], BF16)
        self.w_up_s = it("w_up_s", [L, 128, 44, 8, 128], BF16)
        self.w_dn_s = it("w_dn_s", [L, 128, 8, 22, 128], BF16)
        self.hin16 = self.dram("hin16", [D, S], BF16)
        self.hin32 = self.dram("hin32", [D, T], F32)
        self.h1_16 = self.dram("h1_16", [D, T + 2], BF16)
        self.h1_32 = self.dram("h1_32", [D, T], F32)
        self.SCW = 1 + (T + TW if self.pair else S) + 1
        self.GLW = 15 + (T + TW if self.pair else S) + 15
        self.scch_d = self.dram("scch_d", [256, self.SCW], BF16)
        self.glu_d = self.dram("glu_d", [256, self.GLW], BF16)
        self.oD_d = self.dram("oD_d", [256, T], BF16)
        self.rope_d = self.dram("rope_d", [2, 32, S], BF16)
        if self.pair:
            self.xch_in = [it(f"xch_in{i}", [TW, D], BF16) for i in range(T // TW)]
            self.xch_out = [it(f"xch_out{i}", [2 * TW, D], BF16) for i in range(T // TW)]
            self.e1_in = it("e1_in", [128, 8], F32)
            self.e1_out = it("e1_out", [256, 8], F32)
        sb = lambda n, s, dt=F32: nc.alloc_sbuf_tensor(n, list(s), dt)
        self.consts = sb("consts_sb", [128, NCONST])
        self.vecs = sb("vecs_sb", [128, L, NV])
        self.ident_bf = sb("ident_bf", [128, 128], BF16)
        self.onesD = sb("onesD", [128, 128], BF16)
        self.ones256 = sb("ones256", [128, 128], BF16)
        self.ones128 = sb("ones128", [128, 128], BF16)
        self.bo64 = sb("bo64", [128, 128], BF16)
        self.maskU4 = sb("maskU4", [128, 512], BF16)
        self.maskD4 = sb("maskD4", [128, 512], BF16)
        self.bm = sb("bm", [128, 256], BF16)
        self.cme = sb("cme", [128, 256], BF16)
        self.cmo = sb("cmo", [128, 256], BF16)
        self.sel = sb("sel", [64, 256], BF16)
        if self.pair:
            self.Jsel = sb("Jsel", [128, 2, 128], BF16)
        kv_bytes = 4 * S * 2 + (S // 128) * 4 * 65 * 2
        b2_bytes = 22 * 512 * 2 + 2 * 22 * 128 * 2
        self.kvreg = sb("kvreg", [128, max(kv_bytes, b2_bytes) // 2], BF16)
        o = 0
        self.kc = self.kvreg[0:96, o:o + 4 * S].rearrange("p (h s) -> p h s", h=4)
        o += 4 * S
        self.vaug = self.kvreg[:, o:o + (S // 128) * 260].rearrange("p (b h d) -> p b h d", h=4, d=65)
        self.actb = self.kvreg[:, 0:22 * 512].rearrange("p (m t) -> p m t", m=22)
        o = 22 * 512
        self.wdn = [self.kvreg[:, o + i * 22 * 128:o + (i + 1) * 22 * 128].rearrange("p (m c) -> p m c", m=22)
                    for i in range(2)]
        self.hT = [sb(f"hT{i}", [128, 8, TW + 2], BF16) for i in range(2)]
        self.wbuf = [sb(f"wbuf{i}", [128, 8, 544], BF16) for i in range(3)]
        self.wuq = sb("wuq", [128, 2, 768], BF16)
        self.wukv = sb("wukv", [128, 512], BF16)
        self.wa = sb("wa", [33, 256])
        self.dpool = [sb(f"dg{i}", [128, 128], BF16) for i in range(6)]
        self.big32 = sb("big32", [128, 8, 512])
        self.h32 = sb("h32", [128, 8, 512])
        self.ybuf = sb("ybuf", [128, 8, 512], BF16)
        self.ga_sb = sb("ga_sb", [33, 512])
        self.q_sb = sb("q_sb", [128, 512])
        self.k_sb = sb("k_sb", [128, 512])
        self.long32 = [sb(f"long32_{i}", [128, 512]) for i in range(4)]
        self.yg = [sb(f"yg{i}", [128, 512]) for i in range(2)]
        self.mean_t = sb("mean_t", [128, 512])
        self.var_t = sb("var_t", [128, 512])
        self.cqn = [sb(f"cqn{i}", [128, 512], BF16) for i in range(2)]
        self.win16 = [sb(f"win16_{i}", [128, 544], BF16) for i in range(4)]
        self.ropet = sb("ropet", [96, 2, 512], BF16)
        self.yh = [self.win16[i][0:64, 0:512] for i in range(4)]
        self.tmp32 = [sb(f"tmp32_{i}", [128, 512]) for i in range(4)]
        self.g32 = [sb(f"g32_{i}", [128, 384]) for i in range(4)]
        self.tmp16 = [sb(f"tmp16_{i}", [128, 544], BF16) for i in range(6)]
        self.g16 = [sb(f"g16_{i}", [128, 512], BF16) for i in range(5)]
        self.Sst = sb("Sst", [128, 256])
        self.Sbf = sb("Sbf", [128, 256], BF16)
        self.oDt = sb("oDt", [128, 2, 512], BF16)
        self.qc = [sb(f"qc{i}", [96, 512], BF16) for i in range(2)]
        self.small = sb("small", [128, 16])
        self.tok = sb("tok", [128, 2])
        self.ps = [nc.alloc_psum_tensor(f"ps{i}", [128, 512], F32) for i in range(8)]

    def bank(self, pool=(0, 1, 2)):
        n = self._ctr.get(pool, 0)
        self._ctr[pool] = n + 1
        return pool[n % len(pool)]

    def _rot(self, name, lst):
        n = self._ctr.get(name, 0)
        self._ctr[name] = n + 1
        i = n % len(lst)
        return lst[i], (name, i)

    def t32(self):
        return self._rot("tmp32", self.tmp32)

    def t16(self):
        return self._rot("tmp16", self.tmp16)

    def gt32(self):
        return self._rot("g32", self.g32)

    def gt16(self):
        return self._rot("g16", self.g16)

    def mm(self, out, lhsT, rhs, start, stop, reads, writes):
        nc = self.nc
        self.sc.op("pe", lambda: nc.tensor.matmul(out, lhsT, rhs, start=start, stop=stop),
                   reads=reads, writes=writes)

    def act(self, out, in_, func, reads, writes, bias=None, scale=None):
        nc = self.nc
        kw = {}
        if bias is not None:
            kw["bias"] = bias
        if scale is not None:
            kw["scale"] = scale
        self.sc.op("act", lambda: nc.scalar.activation(out=out, in_=in_, func=func, **kw),
                   reads=reads, writes=writes)

    def tt(self, eng, out, in0, in1, op, reads, writes):
        e = self.sc.engs[eng]
        self.sc.op(eng, lambda: e.tensor_tensor(out=out, in0=in0, in1=in1, op=op),
                   reads=reads, writes=writes)

    def ts(self, eng, out, in0, s1, op0, reads, writes, s2=None, op1=None):
        e = self.sc.engs[eng]
        if op1 is None:
            self.sc.op(eng, lambda: e.tensor_scalar(out=out, in0=in0, scalar1=s1, scalar2=None, op0=op0),
                       reads=reads, writes=writes)
        else:
            self.sc.op(eng, lambda: e.tensor_scalar(out=out, in0=in0, scalar1=s1, scalar2=s2, op0=op0, op1=op1),
                       reads=reads, writes=writes)

    def stt(self, out, in0, scalar, in1, op0, op1, reads, writes):
        nc = self.nc
        self.sc.op("dve", lambda: nc.vector.scalar_tensor_tensor(out=out, in0=in0, scalar=scalar, in1=in1,
                                                                   op0=op0, op1=op1),
                   reads=reads, writes=writes)

    def copy(self, eng, out, in_, reads, writes):
        if eng == "act":
            nc = self.nc
            self.sc.op("act", lambda: nc.scalar.copy(out=out, in_=in_), reads=reads, writes=writes)
        else:
            e = self.sc.engs[eng]
            self.sc.op(eng, lambda: e.tensor_copy(out=out, in_=in_), reads=reads, writes=writes)

    def memset(self, eng, ap, v, writes):
        e = self.sc.engs[eng]
        self.sc.op(eng, lambda: e.memset(ap, v), writes=writes)

    def rstd(self, out, in_, reads, okey):
        self.act(out, in_, AF.Ln, reads, [okey], bias=EPS, scale=1.0)
        self.act(out, out, AF.Exp, [okey], [okey], scale=-0.5)

    def setup(self):
        sc, nc, L = self.sc, self.nc, self.L
        sc.dma("sp", self.consts[:], self.consts_d.ap(), writes=["consts"], key="consts")
        for l in range(L):
            sc.dma("sp", self.vecs[:, l, :], self.vecs_d[l], writes=["vecs"], key="vecs")
        mk = self.big32[:].rearrange("p a b -> p (a b)")[:, 0:NMASK]
        MK = [("big32", i) for i in range(3)]
        self.RG = [[0, 1], [2, 3], [4, 5], [6, 7]]
        sc.dma("sp", mk, self.masks_d.ap(), writes=MK, key="masks")
        cs = self.consts
        self.copy("dve", self.ident_bf[:], cs[:, C_ID:C_ID + 128], ["consts"], ["ident_bf"])
        self.memset("pool", self.onesD[:], 1.0 / 1024, ["onesD"])
        self.memset("pool", self.ones256[:], 1.0 / 256, ["ones256"])
        self.memset("pool", self.ones128[:], 1.0 / 128, ["ones128"])
        self.memset("pool", self.bo64[:], 0.0, ["bo64"])
        self.memset("pool", self.bo64[0:64, 0:64], 1.0 / 64, ["bo64"])
        self.memset("pool", self.bo64[64:128, 64:128], 1.0 / 64, ["bo64"])
        for h in range(4):
            self.copy("dve", self.maskU4[:, h * 128:(h + 1) * 128], mk[:, M_MU:M_MU + 128], MK, ["maskU4"])
            self.copy("dve", self.maskD4[:, h * 128:(h + 1) * 128], mk[:, M_MD:M_MD + 128], MK, ["maskD4"])
        self.copy("dve", self.bm[:], mk[:, M_BM:M_BM + 256], MK, ["bm"])
        self.copy("dve", self.cme[:], mk[:, M_CME:M_CME + 256], MK, ["cme"])
        self.copy("dve", self.cmo[:], mk[:, M_CMO:M_CMO + 256], MK, ["cmo"])
        self.copy("dve", self.sel[:], mk[0:64, M_SEL:M_SEL + 256], MK, ["sel"])
        if self.pair:
            for s_ in range(2):
                self.ts("dve", self.Jsel[:, s_, :], mk[:, M_J:M_J + 128], cs[:, C_SELW + s_:C_SELW + s_ + 1], ALU.mult,
                        MK + ["consts"], ["Jsel"])
        self.memset("pool", self.ga_sb[32:33, :], 1.0, ["ga_ones"])
        zt, zk = self.t16()
        self.memset("pool", zt[:], 0.0, [zk])
        for j in range(2):
            r = slice(j * 128, (j + 1) * 128)
            sc.dma("sp", self.scch_d[r, 0:1], zt[:, 0:1], reads=[zk], writes=["scch_pad"], key="zp", slow=True)
            sc.dma("sp", self.glu_d[r, 0:15], zt[:, 0:15], reads=[zk], writes=["glu_pad"], key="zp", slow=True)
            if not self.pair:
                sc.dma("sp", self.scch_d[r, self.SCW - 1:self.SCW], zt[:, 0:1], reads=[zk], writes=["scch_pad"], key="zp", slow=True)
                sc.dma("sp", self.glu_d[r, self.GLW - 15:self.GLW], zt[:, 0:15], reads=[zk], writes=["glu_pad"], key="zp", slow=True)
        for c in range(8):
            r = slice(c * 128, (c + 1) * 128)
            sc.dma("sp", self.h1_16[r, 0:1], zt[:, 0:1], reads=[zk], writes=["h1padL"], key="zp", slow=True)
            if not self.pair:
                sc.dma("sp", self.h1_16[r, self.T + 1:self.T + 2], zt[:, 0:1], reads=[zk], writes=["h1padR"], key="zp", slow=True)

    def cast_weights(self, l, late=False):
        sc = self.sc
        jobs, names = [], []
        MB = 2 if late else 4
        for nm, src, dst, kc, n in (("in", self.w_in_d, self.w_in_s, 8, NEXT), ("out", self.w_out_d, self.w_out_s, 8, D),
                                    ("uq", self.w_uq_d, self.w_uq_s, 2, 768)):
            for k in range(kc):
                step = n if (not late or n <= 2048) else n // 2
                for c0 in range(0, n, step):
                    jobs.append((src[l, :, k, c0:c0 + step], dst[l, :, k, c0:c0 + step], step, None, 0))
                    names.append(nm)
        jobs.append((self.w_ukv_d[l], self.w_ukv_s[l], 512, None, 0))
        names.append("ukv")
        for m0 in range(0, 44, MB):
            jobs.append((self.w_up_d[l, :, :, m0 * 128:(m0 + MB) * 128], self.w_up_s[l, :, m0:m0 + MB, :, :],
                         MB * 1024, "up", MB))
            names.append("up")
        for mo in range(8):
            for k0 in ((0, 11) if late else (0,)):
                nk = 11 if late else 22
                jobs.append((self.w_dn_d[l, :, k0:k0 + nk, mo * 128:(mo + 1) * 128], self.w_dn_s[l, :, mo, k0:k0 + nk, :],
                             nk * 128, "dn", nk))
                names.append("dn")
        engs = ["act", "act"] if late else ["act", "dve"]
        wk = self.wk.setdefault(l, {})
        if not late:
            in_st = []
            for q in range(min(3, self.kvreg.shape[1] // 8192)):
                in_st.append((self.kvreg[:, q * 8192:(q + 1) * 8192].bitcast(F32), [("kvst", q)]))
            out_st = [(self.wbuf[q][:].rearrange("p a b -> p (a b)"), [("wbuf", q)]) for q in range(3)]
            out_st.append((self.ybuf[:].rearrange("p a b -> p (a b)"), [("ybufst",)]))
        else:
            bf, hf = self.big32[:].rearrange("p a b -> p (a b)"), self.h32[:].rearrange("p a b -> p (a b)")
            yf = self.ybuf[:].rearrange("p a b -> p (a b)")
            in_st = [(bf[:, 0:2048], [("big32", c) for c in range(4)]), (bf[:, 2048:4096], [("big32", c) for c in range(4, 8)]),
                     (hf[:, 0:2048], [("h32", c) for c in range(4)]), (hf[:, 2048:4096], [("h32", c) for c in range(4, 8)])]
            out_st = [(yf[:, 0:2048], [("ybuf", c) for c in range(4)]), (yf[:, 2048:4096], [("ybuf", c) for c in range(4, 8)])]
        base = self._ctr.get("castjob", 0)
        self._ctr["castjob"] = base + len(jobs)
        tag = "L" if late else "E"

        def emit_load(i):
            s_ap, d_ap, n, blk, par = jobs[i]
            f32v, k32s = in_st[(base + i) % len(in_st)]
            dstv = f32v[:, 0:n]
            if blk == "up":
                dstv = dstv.rearrange("p (k c) -> p k c", k=8)
            elif blk == "dn":
                dstv = dstv.rearrange("p (k c) -> p k c", k=par)
            sc.dma(STQ if late else "sp", dstv, s_ap, writes=k32s, key=("st32" + tag, (base + i) % len(in_st)))

        def emit_cast_store(i):
            s_ap, d_ap, n, blk, par = jobs[i]
            gi = base + i
            f32v, k32s = in_st[gi % len(in_st)]
            b16v, k16 = out_st[gi % len(out_st)]
            v32, v16 = f32v[:, 0:n], b16v[:, 0:n]
            if blk == "up":
                self.copy(engs[gi % 2], v16.rearrange("p (m k c) -> p m k c", m=par, k=8),
                          v32.rearrange("p (k m c) -> p m k c", k=8, m=par), k32s, k16)
                src16 = v16.rearrange("p (m k c) -> p m k c", m=par, k=8)
            elif blk == "dn":
                self.copy(engs[gi % 2], v16, v32, k32s, k16)
                src16 = v16.rearrange("p (k c) -> p k c", k=par)
            else:
                self.copy(engs[gi % 2], v16, v32, k32s, k16)
                src16 = v16
            wk.setdefault(names[i], []).append(("wscr", l, i))
            sc.dma(STQ, d_ap, src16, reads=k16, writes=[("wscr", l, i)], key=("st16" + tag, gi % len(out_st)))

        LA = min(3, len(in_st) - 1)
        for j in range(len(jobs) + LA):
            if j < len(jobs):
                emit_load(j)
            if j >= LA:
                emit_cast_store(j - LA)
            yield

    def phase0(self):
        sc, nc, S, T = self.sc, self.nc, self.S, self.T
        cs = self.consts
        flat32 = self.big32[:].rearrange("p a b -> p (a b)")
        hflat = self.h32[:].rearrange("p a b -> p (a b)")
        sm = self.small
        import os
        nblk = int(os.environ.get("P0_BLOCKS", S // 128))
        nostore = os.environ.get("P0_NOSTORE") == "1"
        nocast = os.environ.get("P0_NOCAST") == "1"
        for b in range(nblk):
            xt = flat32[:, 0:1024]
            xn = flat32[:, 1024:2048]
            hb = hflat[:, 0:1024]
            sc.dma("sp", xt, self.x[b * 128:(b + 1) * 128, :], writes=[("big32", 0), ("big32", 1)], key="p0x")
            for i in range(2):
                sc.op("dve", lambda i=i: nc.vector.bn_stats(out=sm[:, 6 * i:6 * i + 6], in_=xt[:, i * 512:(i + 1) * 512]),
                      reads=[("big32", 0), ("big32", 1)], writes=[("p0st", i)])
            sc.op("dve", lambda: nc.vector.bn_aggr(out=sm[:, 12:14], in_=sm[:, 0:12]),
                  reads=[("p0st", 0), ("p0st", 1)], writes=["p0mv"])
            self.act(sm[:, 14:15], sm[:, 13:14], AF.Ln, ["p0mv"], ["p0rs"], bias=EPS, scale=1.0)
            self.act(sm[:, 14:15], sm[:, 14:15], AF.Exp, ["p0rs"], ["p0rs"], scale=-0.5)
            self.stt(sm[:, 15:16], sm[:, 12:13], -1.0, sm[:, 14:15], ALU.mult, ALU.mult, ["p0mv", "p0rs"], ["p0nb"])
            self.act(xn, xt, AF.Identity, [("big32", 0), ("big32", 1), "p0rs", "p0nb"], [("big32", 2), ("big32", 3)], bias=sm[:, 15:16], scale=sm[:, 14:15])
            for g in range(2):
                pb = self.bank()
                for j in range(4):
                    c = g * 4 + j
                    sc.op("pe", lambda c=c, j=j, pb=pb: nc.tensor.transpose(
                        out=self.ps[pb][:, j * 128:(j + 1) * 128], in_=xn[:, c * 128:(c + 1) * 128],
                        identity=cs[:, C_ID:C_ID + 128]), reads=[("big32", 2), ("big32", 3), "consts"], writes=[("ps", pb)])
                for j in range(4):
                    c = g * 4 + j
                    self.act(hb[:, c * 128:(c + 1) * 128], self.ps[pb][:, j * 128:(j + 1) * 128], AF.Identity,
                             [("ps", pb), "consts"], [("h32", c // 4)],
                             bias=cs[:, C_LNIN + 8 + c:C_LNIN + 9 + c], scale=cs[:, C_LNIN + c:C_LNIN + c + 1])
            if nocast:
                continue
            h16 = [self.t16(), self.t16()]
            for g in range(2):
                self.copy("dve" if g else "act", h16[g][0][:, 0:512], hb[:, g * 512:(g + 1) * 512],
                          [("h32", g)], [h16[g][1]])
            yield
            for c in range(8):
                if nostore:
                    break
                src16 = h16[c // 4][0][:, (c % 4) * 128:(c % 4 + 1) * 128]
                sc.dma(STQ, self.hin16[c * 128:(c + 1) * 128, b * 128:(b + 1) * 128], src16,
                       reads=[h16[c // 4][1]], writes=[("hin16", b // 4, b % 4, c)], key=("sto", h16[c // 4][1]))
                if b * 128 < T:
                    sc.dma(STQ, self.hin32[c * 128:(c + 1) * 128, b * 128:(b + 1) * 128], hb[:, c * 128:(c + 1) * 128],
                           reads=[("h32", c // 4)], writes=[("hin32", b // 4, b % 4, c)], key="p0o32")

    def rope_tables(self):
        sc, nc, S = self.sc, self.nc, self.S
        cs = self.consts
        R = slice(64, 96)
        inv = cs[R, C_ROPE:C_ROPE + 1]
        sgn = cs[R, C_ROPE + 1:C_ROPE + 2]
        TWO_PI = 2.0 * math.pi
        C1 = 6.28125
        C2 = TWO_PI - C1
        ang = self.mean_t
        for b in range(S // 512):
            posi, kpi = self.t32()
            src = bass.AP(self.pos, b * 512, [[0, 32], [1, 512]])
            posv = posi[R, :].bitcast(I32)
            sc.dma("sp", posv, src, writes=[kpi], key="pos")
            self.copy("dve", ang[R, :], posv, [kpi], ["mean_t"])
            self.ts("dve", ang[R, :], ang[R, :], inv, ALU.mult, ["mean_t", "consts"], ["mean_t"])
            for ti_, (phase, scale) in enumerate(((math.pi / 2, None), (0.0, sgn))):
                a2, k2 = self.t32()
                qi, kq = self.t32()
                qf, kf = self.t32()
                self.ts("dve", a2[R, :], ang[R, :], phase, ALU.add, ["mean_t"], [k2])
                qiv = qi[R, :].bitcast(I32)
                self.ts("dve", qiv, a2[R, :], 1.0 / TWO_PI, ALU.mult, [k2], [kq])
                self.copy("dve", qf[R, :], qiv, [kq], [kf])
                self.stt(a2[R, :], qf[R, :], -C1, a2[R, :], ALU.mult, ALU.add, [kf, k2], [k2])
                self.stt(a2[R, :], qf[R, :], -C2, a2[R, :], ALU.mult, ALU.add, [kf, k2], [k2])
                self.ts("dve", qf[R, :], a2[R, :], math.pi, ALU.is_gt, [k2], [kf])
                self.stt(a2[R, :], qf[R, :], -TWO_PI, a2[R, :], ALU.mult, ALU.add, [kf, k2], [k2])
                self.ts("dve", qf[R, :], a2[R, :], -math.pi, ALU.is_lt, [k2], [kf])
                self.stt(a2[R, :], qf[R, :], TWO_PI, a2[R, :], ALU.mult, ALU.add, [kf, k2], [k2])
                self.ts("dve", a2[R, :], a2[R, :], math.pi, ALU.min, [k2], [k2], s2=-math.pi, op1=ALU.max)
                o16, ko = self.t16()
                if scale is None:
                    self.act(o16[R, 0:512], a2[R, :], AF.Sin, [k2], [ko])
                else:
                    self.act(o16[R, 0:512], a2[R, :], AF.Sin, [k2, "consts"], [ko], scale=scale)
                sc.dma(STQ, self.rope_d[ti_, :, b * 512:(b + 1) * 512], o16[R, 0:512], reads=[ko],
                       writes=[("rope", b, ti_)], key=("sto", ko))
            yield

    def load_rope(self, ti):
        for i in range(2):
            self.sc.dma("sp", self.ropet[64:96, i, :], self.rope_d[i, :, ti * TW:(ti + 1) * TW],
                        reads=[("rope", ti, i)], writes=[("ropet", i)], key=("ropet", i))

    def load_hT(self, ti, slot):
        t0 = ti * TW
        self.sc.dma("sp", self.hT[slot][:, :, 0:TW], self.hin16[:, t0:t0 + TW].rearrange("(c p) t -> p c t", p=128),
                    reads=[("hin16", ti, bb, kc) for bb in range(4) for kc in range(8)],
                    writes=[("hT", slot, kc) for kc in range(8)], key=("hT", slot))

    def load_w(self, scr, l, c0, n, slot):
        self.sc.dma("sp", self.wbuf[slot][:, :, 0:n], scr[l, :, :, c0:c0 + n],
                    reads=self.wk[l]["in" if scr is self.w_in_s else "out"],
                    writes=[("wbuf", slot)], key=("wbuf", slot))

    def proj_fm(self, pb, M, slot, c0, hslot):
        for kc in range(8):
            self.mm(self.ps[pb][0:M, :], self.wbuf[slot][:, kc, c0:c0 + M], self.hT[hslot][:, kc, 0:TW],
                    kc == 0, kc == 7, [("wbuf", slot), ("hT", hslot, kc)], [("ps", pb)])

    def diag_tile(self, l, vcol):
        dg, kdg = self._rot("dg", self.dpool)
        if kdg[1] % 2 == 0:
            self.ts("dve", dg[:], self.ident_bf[:], self.vecs[:, l, vcol:vcol + 1], ALU.mult, ["ident_bf", "vecs"], [kdg])
        else:
            self.act(dg[:], self.ident_bf[:], AF.Identity, ["ident_bf", "vecs"], [kdg], scale=self.vecs[:, l, vcol:vcol + 1])
        return dg, kdg

    def merge(self, *gens):
        gens = list(gens)
        while gens:
            for g_ in list(gens):
                try:
                    next(g_)
                except StopIteration:
                    gens.remove(g_)

    def gla_chunk(self, X, sub, with_out, hslot, wslot):
        GP = self.GP
        cs = self.consts
        cols = slice(sub * 128, (sub + 1) * 128)
        c_tri = C_TRIU if X == 0 else C_TRID
        tri = cs[:, c_tri:c_tri + 128]
        mask4 = self.maskU4 if X == 0 else self.maskD4
        mkey = "maskU4" if X == 0 else "maskD4"
        pkv = self.bank(GP)
        for kc in range(8):
            self.mm(self.ps[pkv][:, 0:384], self.hT[hslot][:, kc, cols], self.wbuf[wslot][:, kc, 128:512],
                    kc == 0, kc == 7, [("hT", hslot, kc), ("wbuf", wslot)], [("ps", pkv)])
        pla = self.bank(GP)
        self.mm(self.ps[pla][:, 0:128], self.ga_sb[0:33, cols], self.wa[0:33, X * 128:(X + 1) * 128], True, True,
                ["ga_sb", "ga_ones", "wa"], [("ps", pla)])
        yield
        lp, klp = self.gt32()
        self.act(lp[:, 0:128], self.ps[pla][:, 0:128], AF.Exp, [("ps", pla)], [klp], scale=-1.0)
        self.act(lp[:, 0:128], lp[:, 0:128], AF.Ln, [klp], [klp], bias=1.0, scale=1.0)
        yield
        pB = self.bank(GP)
        self.mm(self.ps[pB][:, 0:128], tri, lp[:, 0:128], True, True, ["consts", klp], [("ps", pB)])
        self.mm(self.ps[pB][:, 128:256], lp[:, 0:128], tri, True, True, ["consts", klp], [("ps", pB)])
        yield
        E, kE = self.gt32()
        self.act(E[:, 0:128], self.ps[pB][:, 0:128], AF.Exp, [("ps", pB)], [(kE, 0)], scale=-1.0)
        self.act(E[:, 128:256], self.ps[pB][:, 128:256], AF.Exp, [("ps", pB)], [(kE, 1)], scale=1.0)
        self.act(E[:, 256:384], self.ps[pB][:, 128:256], AF.Exp, [("ps", pB)], [(kE, 2)], scale=-1.0)
        yield
        V, kV = self.gt16()
        self.tt("dve", V[:, 0:256], self.ps[pkv][:, 128:384], self.cme[:], ALU.mult, [("ps", pkv), "cme"], [(kV, 0)])
        self.tt("dve", V[:, 256:512], self.ps[pkv][:, 128:384], self.cmo[:], ALU.mult, [("ps", pkv), "cmo"], [(kV, 1)])
        Kt, kKt = self.gt16()
        self.tt("dve", Kt[:, 0:128], self.ps[pkv][:, 0:128], E[:, 0:128], ALU.mult, [("ps", pkv), (kE, 0)], [kKt])
        po = None
        if with_out:
            Q, kQ = self.gt16()
            self.stt(Q[:, 0:128], self.q_sb[:, cols], 32.0 ** -0.5, E[:, 128:256], ALU.mult, ALU.mult,
                     ["q_sb", (kE, 1)], [kQ])
            Kh, kKh = self.gt16()
            for h in range(4):
                self.stt(Kh[:, h * 128:(h + 1) * 128], self.k_sb[:, cols], cs[:, C_HM + h:C_HM + h + 1], E[:, 256:384],
                         ALU.mult, ALU.mult, ["k_sb", (kE, 2), "consts"], [(kKh, h)])
            yield
            psc = self.bank(GP)
            for h in range(4):
                self.mm(self.ps[psc][:, h * 128:(h + 1) * 128], Kh[:, h * 128:(h + 1) * 128], Q[:, 0:128], True, True,
                        [(kKh, h), kQ], [("ps", psc)])
            yield
            A, kA = self.gt16()
            self.tt("dve", A[:, 0:512], self.ps[psc][:, 0:512], mask4[:], ALU.mult, [("ps", psc), mkey], [kA])
            yield
            po = self.bank(GP)
            for j in range(2):
                o = self.ps[po][:, j * 128:(j + 1) * 128]
                self.mm(o, V[:, j * 128:(j + 1) * 128], A[:, (2 * j) * 128:(2 * j + 1) * 128], True, False,
                        [(kV, 0), kA], [("ps", po)])
                self.mm(o, V[:, 256 + j * 128:256 + (j + 1) * 128], A[:, (2 * j + 1) * 128:(2 * j + 2) * 128], False, False,
                        [(kV, 1), kA], [("ps", po)])
                self.mm(o, self.Sbf[:, j * 128:(j + 1) * 128], Q[:, 0:128], False, True, ["Sbf", kQ], [("ps", po)])
        yield
        pdS = self.bank(GP)
        self.mm(self.ps[pdS][:, 0:256], Kt[:, 0:128], V[:, 0:256], True, False, [kKt, (kV, 0)], [("ps", pdS)])
        self.mm(self.ps[pdS][:, 0:256], Kt[:, 0:128], V[:, 256:512], False, True, [kKt, (kV, 1)], [("ps", pdS)])
        yield
        ecol = E[:, 255:256] if X == 0 else E[:, 128:129]
        S1, kS1 = self.gt32()
        self.stt(S1[:, 0:256], self.ps[pdS][:, 0:256], ecol, self.bm[:], ALU.mult, ALU.mult,
                 [("ps", pdS), (kE, 1), "bm"], [kS1])
        self.stt(self.Sst[:], self.Sst[:], ecol, S1[:, 0:256], ALU.mult, ALU.add, ["Sst", (kE, 1), kS1], ["Sst"])
        self.copy("act", self.Sbf[:], self.Sst[:], ["Sst"], ["Sbf"])
        self._po = po
        yield

    def sweepA_tile(self, l, ti, extra=None):
        sc, T = self.sc, self.T
        t0 = ti * TW
        own = t0 < T
        need_conv = t0 < (T + TW if self.pair else self.S)
        hs = ti % 2
        vec = self.vecs
        tcols = slice(t0, t0 + TW)
        R = slice(64, 96)
        self.load_hT(ti, hs)
        self.load_rope(ti)
        self.load_w(self.w_in_s, l, G2, 320, 0)
        self.load_w(self.w_in_s, l, G1, 544, 1)
        pga = self.bank()
        self.proj_fm(pga, 32, 1, 512, hs)
        self.copy("act", self.ga_sb[0:32, :], self.ps[pga][0:32, :], [("ps", pga)], ["ga_sb"])
        if own:
            pq, pk2 = self.bank(), self.bank()
            self.proj_fm(pq, 128, 1, 0, hs)
            self.proj_fm(pk2, 128, 1, 128, hs)
            self.copy("act", self.q_sb[:], self.ps[pq][:, :], [("ps", pq)], ["q_sb"])
            self.copy("dve", self.k_sb[:], self.ps[pk2][:, :], [("ps", pk2)], ["k_sb"])

        def main():
            pck = self.bank()
            self.proj_fm(pck, 128, 0, 0, hs)
            sq, ksq = self.t16()
            ck, kck = self.t32()
            self.act(sq[:, 0:512], self.ps[pck][:, :], AF.Square, [("ps", pck)], [ksq])
            self.copy("dve", ck[:], self.ps[pck][:, :], [("ps", pck)], [kck])
            yield
            pms = self.bank()
            self.mm(self.ps[pms][:, :], self.ones128[:], sq[:, 0:512], True, True, ["ones128", ksq], [("ps", pms)])
            rs, krs = self.t32()
            self.rstd(rs[:], self.ps[pms][:, :], [("ps", pms)], krs)
            ckn, kckn = self.t16()
            self.stt(ckn[:, 0:512], ck[:], vec[:, l, V_GKV:V_GKV + 1], rs[:], ALU.mult, ALU.mult, [kck, krs, "vecs"], [kckn])
            yield
            for h in range(4):
                pk = self.bank()
                self.mm(self.ps[pk][0:64, :], self.wukv[:, h * 64:(h + 1) * 64], ckn[:, 0:512], True, True,
                        ["wukv", kckn], [("ps", pk)])
                self.copy("act" if h % 2 == 0 else "dve", self.kc[0:64, h, tcols], self.ps[pk][0:64, :],
                          [("ps", pk)], [("kc", ti, h)])
                yield
            for sub in range(4):
                pv = self.bank()
                self.mm(self.ps[pv][:, 0:256], ckn[:, sub * 128:(sub + 1) * 128], self.wukv[:, 256:512], True, True,
                        ["wukv", kckn], [("ps", pv)])
                blk = ti * 4 + sub
                self.copy("dve" if sub % 2 == 0 else "act", self.vaug[:, blk, :, 0:64],
                          self.ps[pv][:, 0:256].rearrange("p (h d) -> p h d", h=4), [("ps", pv)], [("vaug", ti)])
                yield
            self.memset("dve", self.vaug[:, ti * 4:ti * 4 + 4, :, 64:65], 1.0, [("vaug1", ti)])
            pka = self.bank()
            self.proj_fm(pka, 96, 0, 128, hs)
            r1, kr1 = self.t32()
            self.tt("dve", r1[R, :], self.ps[pka][R, :], self.ropet[R, 0, :], ALU.mult, [("ps", pka), ("ropet", 0)], [kr1])
            yield
            pkb = self.bank()
            self.proj_fm(pkb, 96, 0, 224, hs)
            r2, kr2 = self.t32()
            self.tt("dve", r2[R, :], self.ps[pkb][R, :], self.ropet[R, 1, :], ALU.mult, [("ps", pkb), ("ropet", 1)], [kr2])
            self.tt("dve", self.kc[R, 0, tcols], r1[R, :], r2[R, :], ALU.add, [kr1, kr2], [("kcr", ti, 0)])
            for h in range(1, 4):
                self.copy("act" if h % 2 else "dve", self.kc[R, h, tcols], self.kc[R, 0, tcols],
                          [("kcr", ti, 0)], [("kcr", ti, h)])
            yield
            if need_conv:
                self.load_w(self.w_in_s, l, G3, 512, 2)
                for j in range(2):
                    pc_, ph_ = self.bank(), self.bank()
                    self.proj_fm(pc_, 128, 2, j * 128, hs)
                    self.proj_fm(ph_, 128, 2, 256 + j * 128, hs)
                    tmp, ktmp = self.t32()
                    self.copy("act", tmp[:], self.ps[pc_][:, :], [("ps", pc_)], [ktmp])
                    o16, ko16 = self.t16()
                    self.tt("dve", o16[:, 0:512], tmp[:], self.ps[ph_][:, :], ALU.mult, [ktmp, ("ps", ph_)], [ko16])
                    sc.dma(STQ, self.scch_d[j * 128:(j + 1) * 128, 1 + t0:1 + t0 + TW], o16[:, 0:512],
                           reads=[ko16], writes=[("scch", ti, j)], key=("sto", ko16))
                    yield
                self.load_w(self.w_in_s, l, G4, 512, 0)
                for j in range(2):
                    pa_, pg_ = self.bank(), self.bank()
                    self.proj_fm(pa_, 128, 0, j * 128, hs)
                    self.proj_fm(pg_, 128, 0, 256 + j * 128, hs)
                    tmp, ktmp = self.t32()
                    self.act(tmp[:], self.ps[pg_][:, :], AF.Sigmoid, [("ps", pg_)], [ktmp])
                    o16, ko16 = self.t16()
                    self.tt("dve", o16[:, 0:512], tmp[:], self.ps[pa_][:, :], ALU.mult, [ktmp, ("ps", pa_)], [ko16])
                    sc.dma(STQ, self.glu_d[j * 128:(j + 1) * 128, 15 + t0:15 + t0 + TW], o16[:, 0:512],
                           reads=[ko16], writes=[("glu", ti, j)], key=("sto", ko16))
                    yield

        def side():
            for sub in (3, 2, 1, 0):
                yield from self.gla_chunk(1, sub, own, hs, 1)
                if own:
                    po = self._po
                    self.copy("act", self.oDt[:, :, sub * 128:(sub + 1) * 128],
                              self.ps[po][:, 0:256].rearrange("p (j t) -> p j t", j=2), [("ps", po)], [("oDt", 0), ("oDt", 1)])
            if own:
                for j in range(2):
                    sc.dma(STQ, self.oD_d[j * 128:(j + 1) * 128, tcols], self.oDt[:, j, :],
                           reads=[("oDt", j)], writes=[("oD", ti, j)], key="oDst")

        def extra_slice():
            for _ in range(self.late_per_tile):
                try:
                    next(extra)
                except StopIteration:
                    return
                yield

        if extra is None:
            self.merge(main(), side())
        else:
            self.merge(main(), side(), extra_slice())

    def layer_weights_small(self, l):
        sc = self.sc
        sc.dma("sp", self.wuq[:], self.w_uq_s[l], reads=self.wk[l]["uq"], writes=["wuq"], key="wuq")
        sc.dma("sp", self.wukv[:], self.w_ukv_s[l], reads=self.wk[l]["ukv"], writes=["wukv"], key="wukv")
        sc.dma("sp", self.wa[:], self.w_a_d[l], writes=["wa"], key="wa")

    def group_rms(self, l, g):
        pms = self.bank()
        for j in range(2):
            sq, ksq = self.t16()
            self.act(sq[:, 0:512], self.yg[j][:], AF.Square, [("yg", j)], [ksq])
            self.mm(self.ps[pms][:, :], self.ones256[:], sq[:, 0:512], j == 0, j == 1, ["ones256", ksq], [("ps", pms)])
        rs, krs = self.t32()
        self.rstd(rs[:], self.ps[pms][:, :], [("ps", pms)], krs)
        for j in range(2):
            c = V_GBR + 2 * g + j
            self.stt(self.ybuf[:, 2 * g + j, :], self.yg[j][:], self.vecs[:, l, c:c + 1], rs[:],
                     ALU.mult, ALU.mult, [("yg", j), krs, "vecs"], [("ybuf", 2 * g + j)])

    def layernorm8(self, l, gcol, bcol, mid=None):
        r = self.big32
        pm, pq = self.bank((6, 7)), self.bank((6, 7))
        for m in range(8):
            rb, krb = self.t16()
            rq, krq = self.t16()
            self.copy("dve" if m % 2 else "act", rb[:, 0:512], r[:, m, :], [("big32", m)], [krb])
            self.act(rq[:, 0:512], r[:, m, :], AF.Square, [("big32", m)], [krq])
            self.mm(self.ps[pm][:, :], self.onesD[:], rb[:, 0:512], m == 0, m == 7, ["onesD", krb], [("ps", pm)])
            self.mm(self.ps[pq][:, :], self.onesD[:], rq[:, 0:512], m == 0, m == 7, ["onesD", krq], [("ps", pq)])
        mean, var = self.mean_t, self.var_t
        msq, kmsq = self.t32()
        self.copy("dve", mean[:], self.ps[pm][:, :], [("ps", pm)], ["mean_t"])
        self.act(msq[:], self.ps[pm][:, :], AF.Square, [("ps", pm)], [kmsq])
        self.tt("dve", var[:], self.ps[pq][:, :], msq[:], ALU.subtract, [("ps", pq), kmsq], ["var_t"])
        self.rstd(var[:], var[:], ["var_t"], "var_t")
        if mid is not None:
            mid()
        for m in range(8):
            t, kt = self.t32()
            self.tt("dve", t[:], r[:, m, :], mean[:], ALU.subtract, [("big32", m), "mean_t"], [kt])
            self.stt(t[:], t[:], self.vecs[:, l, gcol + m:gcol + m + 1], var[:], ALU.mult, ALU.mult, [kt, "var_t", "vecs"], [kt])
            self.act(self.h32[:, m, :], t[:], AF.Identity, [kt, "vecs"], [("h32", m)],
                     bias=self.vecs[:, l, bcol + m:bcol + m + 1], scale=1.0)

    def mla_gen(self, l, ti, hs):
        sc, cs, S = self.sc, self.consts, self.S
        vec = self.vecs
        t0 = ti * TW
        tcols = slice(t0, t0 + TW)
        self.load_w(self.w_in_s, l, B2G, 256, 1)
        cq = self.long32[2:4]
        pms = self.bank((6, 7))
        for j in range(2):
            pc_ = self.bank()
            self.proj_fm(pc_, 128, 1, j * 128, hs)
            self.copy("dve", cq[j][:], self.ps[pc_][:, :], [("ps", pc_)], [("long32", 2 + j)])
            sq, ksq = self.t16()
            self.act(sq[:, 0:512], self.ps[pc_][:, :], AF.Square, [("ps", pc_)], [ksq])
            self.mm(self.ps[pms][:, :], self.ones256[:], sq[:, 0:512], j == 0, j == 1, ["ones256", ksq], [("ps", pms)])
        rs, krs = self.t32()
        self.rstd(rs[:], self.ps[pms][:, :], [("ps", pms)], krs)
        for j in range(2):
            self.stt(self.cqn[j][:], cq[j][:], vec[:, l, V_GQ + j:V_GQ + j + 1], rs[:], ALU.mult, ALU.mult,
                     [("long32", 2 + j), krs, "vecs"], [("cqn", j)])
        R = slice(64, 96)
        scale = 96.0 ** -0.5
        nblk = S // 128
        for h in range(4):
            qc = self.qc[h % 2]
            kqc = ("qc", h % 2)
            pqa, pqb = self.bank(), self.bank()
            for j in range(2):
                self.mm(self.ps[pqa][0:96, :], self.wuq[:, j, h * 96:(h + 1) * 96], self.cqn[j][:], j == 0, j == 1,
                        ["wuq", ("cqn", j)], [("ps", pqa)])
            for j in range(2):
                self.mm(self.ps[pqb][0:96, :], self.wuq[:, j, 384 + h * 96:384 + (h + 1) * 96], self.cqn[j][:],
                        j == 0, j == 1, ["wuq", ("cqn", j)], [("ps", pqb)])
            self.copy("act", qc[0:64, :], self.ps[pqa][0:64, :], [("ps", pqa)], [(kqc, 0)])
            r1, kr1 = self.t32()
            r2, kr2 = self.t32()
            self.tt("dve", r1[R, :], self.ps[pqa][R, :], self.ropet[R, 0, :], ALU.mult, [("ps", pqa), ("ropet", 0)], [kr1])
            self.tt("dve", r2[R, :], self.ps[pqb][R, :], self.ropet[R, 1, :], ALU.mult, [("ps", pqb), ("ropet", 1)], [kr2])
            self.tt("dve", qc[R, :], r1[R, :], r2[R, :], ALU.add, [kr1, kr2], [(kqc, 1)])
            pacc = self.bank((6, 7))
            LAG = 2
            pend = []
            for blk in range(nblk + LAG):
                if blk < nblk:
                    pss = self.bank(self.SCP)
                    kti = blk // 4
                    self.mm(self.ps[pss][:, :], self.kc[0:96, h, blk * 128:(blk + 1) * 128], qc[0:96, :], True, True,
                            [("kc", kti, h), ("kcr", kti, h), (kqc, 0), (kqc, 1), "kvrd"], [("ps", pss)])
                    P, kP = self.t16()
                    self.act(P[:, 0:512], self.ps[pss][:, :], AF.Exp, [("ps", pss)], [kP], scale=scale)
                    pend.append((blk, P, kP))
                yield
                if blk >= LAG:
                    b2, P, kP = pend.pop(0)
                    kti = b2 // 4
                    self.mm(self.ps[pacc][0:65, :], self.vaug[:, b2, h, 0:65], P[:, 0:512], b2 == 0, b2 == nblk - 1,
                            [("vaug", kti), ("vaug1", kti), kP, "kvrd"], [("ps", pacc)])
            osb, kosb = self.t32()
            self.copy("dve", osb[0:65, :], self.ps[pacc][0:65, :], [("ps", pacc)], [kosb])
            pden = self.bank()
            self.mm(self.ps[pden][0:64, :], cs[64:65, C_ONES:C_ONES + 64], osb[64:65, :], True, True,
                    ["consts", kosb], [("ps", pden)])
            rd, krd = self.t32()
            self.act(rd[0:64, :], self.ps[pden][0:64, :], AF.Ln, [("ps", pden)], [krd])
            self.act(rd[0:64, :], rd[0:64, :], AF.Exp, [krd], [krd], scale=-1.0)
            self.tt("dve", self.yh[h], osb[0:64, :], rd[0:64, :], ALU.mult, [kosb, krd], [("win16", h)])
        yield
        for j in range(2):
            pyc = self.bank()
            self.mm(self.ps[pyc][:, :], self.sel[:, 0:128], self.yh[2 * j], True, False, ["sel", ("win16", 2 * j)], [("ps", pyc)])
            self.mm(self.ps[pyc][:, :], self.sel[:, 128:256], self.yh[2 * j + 1], False, True,
                    ["sel", ("win16", 2 * j + 1)], [("ps", pyc)])
            self.copy("act", self.yg[j][:], self.ps[pyc][:, :], [("ps", pyc)], [("yg", j)])
        self.group_rms(l, 2)

    def b1_loads(self, l, ti):
        sc = self.sc
        t0 = ti * TW
        tcols = slice(t0, t0 + TW)
        self.pref.add(("b1", l, ti))
        self.load_hT(ti, ti % 2)
        self.load_rope(ti)
        nb = lambda key, j: [(key, ti, j), (key, min(ti + 1, self.NT - 1), j), (key, max(ti - 1, 0), j), key + "_pad"]
        for j in range(2):
            sc.dma("sp", self.win16[j][:, 0:TW + 2], self.scch_d[j * 128:(j + 1) * 128, t0:t0 + TW + 2],
                   reads=nb("scch", j), writes=[("win16", j)], key=("win16", j))
            sc.dma("sp", self.win16[2 + j][:, 0:TW + 30], self.glu_d[j * 128:(j + 1) * 128, t0:t0 + TW + 30],
                   reads=nb("glu", j), writes=[("win16", 2 + j)], key=("win16", 2 + j))
            sc.dma("sp", self.oDt[:, j, :], self.oD_d[j * 128:(j + 1) * 128, tcols], reads=[("oD", ti, j)],
                   writes=[("oDt", j)], key=("oDld", j))
        self.load_w(self.w_in_s, l, G1, 544, 0)

    def sweepB1_tile(self, l, ti):
        sc, cs, S = self.sc, self.consts, self.S
        vec = self.vecs
        t0 = ti * TW
        tcols = slice(t0, t0 + TW)
        hs = ti % 2
        if ("b1", l, ti) not in self.pref:
            self.b1_loads(l, ti)
        pga = self.bank()
        self.proj_fm(pga, 32, 0, 512, hs)
        self.copy("act", self.ga_sb[0:32, :], self.ps[pga][0:32, :], [("ps", pga)], ["ga_sb"])
        pq, pk2 = self.bank(), self.bank()
        self.proj_fm(pq, 128, 0, 0, hs)
        self.proj_fm(pk2, 128, 0, 128, hs)
        self.copy("act", self.q_sb[:], self.ps[pq][:, :], [("ps", pq)], ["q_sb"])
        self.copy("dve", self.k_sb[:], self.ps[pk2][:, :], [("ps", pk2)], ["k_sb"])
        self.load_w(self.w_in_s, l, B1G, 512, 2)
        for j in range(2):
            w, kw = self.win16[j], ("win16", j)
            acc, kacc = self.t32()
            wcol = lambda k: vec[:, l, V_SCW + k * 2 + j:V_SCW + k * 2 + j + 1]
            self.ts("dve", acc[:], w[:, 0:TW], wcol(0), ALU.mult, [kw, "vecs"], [kacc])
            self.stt(acc[:], w[:, 1:TW + 1], wcol(1), acc[:], ALU.mult, ALU.add, [kw, kacc, "vecs"], [kacc])
            self.stt(acc[:], w[:, 2:TW + 2], wcol(2), acc[:], ALU.mult, ALU.add, [kw, kacc, "vecs"], [kacc])
            pb_ = self.bank()
            self.proj_fm(pb_, 128, 2, j * 128, hs)
            self.tt("dve", self.yg[j][:], acc[:], self.ps[pb_][:, :], ALU.mult, [kacc, ("ps", pb_)], [("yg", j)])
        self.group_rms(l, 0)
        ud = self.long32[2:4]
        pm_, pq_ = self.bank((6, 7)), self.bank((6, 7))
        for j in range(2):
            w, kw = self.win16[2 + j], ("win16", 2 + j)
            pcf = self.bank()
            for k in range(31):
                dg, kdg = self.diag_tile(l, V_CFW + k * 2 + j)
                self.mm(self.ps[pcf][:, :], dg[:], w[:, k:k + TW], k == 0, k == 30, [kdg, kw], [("ps", pcf)])
            self.copy("dve", ud[j][:], self.ps[pcf][:, :], [("ps", pcf)], [("long32", 2 + j)])
            ub, kub = self.t16()
            uq, kuq = self.t16()
            self.copy("act", ub[:, 0:512], self.ps[pcf][:, :], [("ps", pcf)], [kub])
            self.act(uq[:, 0:512], self.ps[pcf][:, :], AF.Square, [("ps", pcf)], [kuq])
            self.mm(self.ps[pm_][:, :], self.ones256[:], ub[:, 0:512], j == 0, j == 1, ["ones256", kub], [("ps", pm_)])
            self.mm(self.ps[pq_][:, :], self.ones256[:], uq[:, 0:512], j == 0, j == 1, ["ones256", kuq], [("ps", pq_)])
        mean, var = self.mean_t, self.var_t
        msq, kmsq = self.t32()
        self.copy("dve", mean[:], self.ps[pm_][:, :], [("ps", pm_)], ["mean_t"])
        self.act(msq[:], self.ps[pm_][:, :], AF.Square, [("ps", pm_)], [kmsq])
        self.tt("dve", var[:], self.ps[pq_][:, :], msq[:], ALU.subtract, [("ps", pq_), kmsq], ["var_t"])
        self.rstd(var[:], var[:], ["var_t"], "var_t")
        for j in range(2):
            u, ku = ud[j], ("long32", 2 + j)
            self.tt("dve", u[:], u[:], mean[:], ALU.subtract, [ku, "mean_t"], [ku])
            self.stt(u[:], u[:], vec[:, l, V_CFG + j:V_CFG + j + 1], var[:], ALU.mult, ALU.mult, [ku, "var_t", "vecs"], [ku])
            self.act(self.yg[j][:], u[:], AF.Silu, [ku, "vecs"], [("yg", j)],
                     bias=vec[:, l, V_CFB + j:V_CFB + j + 1], scale=1.0)
        self.group_rms(l, 3)
        og = self.long32[0:2]

        def side():
            for sub in range(4):
                yield from self.gla_chunk(0, sub, True, hs, 0)
                po = self._po
                for j in range(2):
                    self.tt("dve", og[j][:, sub * 128:(sub + 1) * 128], self.ps[po][:, j * 128:(j + 1) * 128],
                            self.oDt[:, j, sub * 128:(sub + 1) * 128], ALU.add, [("ps", po), ("oDt", j)], [("long32", j)])

        self.merge(self.mla_gen(l, ti, hs), side())
        for j in range(2):
            sq, ksq = self.t16()
            self.act(sq[:, 0:512], og[j][:], AF.Square, [("long32", j)], [ksq])
            pms = self.bank()
            self.mm(self.ps[pms][:, :], self.bo64[:], sq[:, 0:512], True, True, ["bo64", ksq], [("ps", pms)])
            rs, krs = self.t32()
            self.rstd(rs[:], self.ps[pms][:, :], [("ps", pms)], krs)
            pgg = self.bank()
            self.proj_fm(pgg, 128, 2, 256 + j * 128, hs)
            sg, ksg = self.t32()
            self.act(sg[:], self.ps[pgg][:, :], AF.Silu, [("ps", pgg)], [ksg])
            self.stt(rs[:], og[j][:], vec[:, l, V_GH + j:V_GH + j + 1], rs[:], ALU.mult, ALU.mult,
                     [("long32", j), krs, "vecs"], [krs])
            self.tt("dve", self.yg[j][:], rs[:], sg[:], ALU.mult, [krs, ksg], [("yg", j)])
        self.group_rms(l, 1)
        sc.dma("sp", self.h32[:, :, :], self.hin32[:, tcols].rearrange("(c p) t -> p c t", p=128),
               reads=[("hin32", ti, bb, c) for bb in range(4) for c in range(8)], writes=[("h32", c) for c in range(8)],
               key="h32ld")
        for half in range(2):
            self.load_w(self.w_out_s, l, half * 512, 512, 1 + half)
        if ti + 1 < self.NTO:
            self.b1_loads(l, ti + 1)
        elif self.NTO >= 3:
            self.b2_loads(l, 0)
            self.w_up_load(l, 0, 0)
        for half in range(2):
            for mm_ in range(4):
                m = half * 4 + mm_
                pmx = self.bank()
                for kc in range(8):
                    self.mm(self.ps[pmx][:, :], self.wbuf[1 + half][:, kc, mm_ * 128:(mm_ + 1) * 128], self.ybuf[:, kc, :],
                            kc == 0, kc == 7, [("wbuf", 1 + half), ("ybuf", kc)], [("ps", pmx)])
                self.stt(self.big32[:, m, :], self.h32[:, m, :], ALPHA, self.ps[pmx][:, :], ALU.mult, ALU.add,
                         [("h32", m), ("ps", pmx)], [("big32", m)])
        self.layernorm8(l, V_L1G, V_L1B)
        for m in range(8):
            self.copy("dve", self.ybuf[:, m, :], self.h32[:, m, :], [("h32", m)], [("ybuf", m)])
        sc.dma(STQ, self.h1_16[:, 1 + t0:1 + t0 + TW].rearrange("(c p) t -> p c t", p=128), self.ybuf[:, 0:8, :],
               reads=[("ybuf", m) for m in range(8)], writes=[("h1_16", ti, m) for m in range(8)], key="h1_16st")
        sc.dma(STQ, self.h1_32[:, tcols].rearrange("(c p) t -> p c t", p=128), self.h32[:, :, :],
               reads=[("h32", m) for m in range(8)], writes=[("h1_32", ti, m) for m in range(8)], key="h1_32st")

    def b2_loads(self, l, tj):
        sc = self.sc
        self.pref.add(("b2", l, tj))
        h1deps = []
        for kc in range(8):
            h1deps += [("h1_16", tj, kc), ("h1_16", max(tj - 1, 0), kc), ("h1_16", min(tj + 1, self.NTO - 1), kc)]
        if tj == 0:
            h1deps.append("h1padL")
        if tj == self.NTO - 1:
            h1deps.append("h1padR")
        sc.dma("sp", self.hT[tj % 2][:, :, :], self.h1_16[:, tj * TW:tj * TW + TW + 2].rearrange("(c p) t -> p c t", p=128),
               reads=h1deps, writes=[("hT", tj % 2, kc) for kc in range(8)], key=("hT", tj % 2))

    def w_up_load(self, l, tj, m):
        sc = self.sc
        slot = m % 3
        self.pref.add(("b2w", l, tj, m))
        sc.dma("sp", self.wbuf[slot][:, :, 0:128], self.w_up_s[l, :, m, :, :],
               reads=self.wk[l]["up"], writes=[("wbufB", slot, 0), ("wbuf", slot)], key=("wbuf", slot, 0))
        sc.dma("sp", self.wbuf[slot][:, :, 128:256], self.w_up_s[l, :, 22 + m, :, :],
               reads=self.wk[l]["up"], writes=[("wbufB", slot, 1), ("wbuf", slot)], key=("wbuf", slot, 1))

    def b2_S1(self, l, ti, i, pool):
        sc = self.sc
        hs = ti % 2
        st = self.b2st.setdefault((l, ti), {"ust": {}, "pcs": {}, "pre": 0})
        m, which = i // 2, i % 2
        slot = m % 3
        halo = lambda kc: self.hT[hs][:, kc, 0:TW + 2:TW + 1]
        if which == 0 and ("b2w", l, ti, m) not in self.pref:
            self.w_up_load(l, ti, m)
        pu, puh = self.bank(pool), self.bank(pool)
        for kc in range(8):
            self.mm(self.ps[pu][:, :], self.wbuf[slot][:, kc, which * 128:(which + 1) * 128],
                    self.hT[hs][:, kc, 1:TW + 1], kc == 0, kc == 7, [("wbufB", slot, which), ("wbuf", slot), ("hT", hs, kc)], [("ps", pu)])
        for kc in range(8):
            self.mm(self.ps[puh][:, 0:2], self.wbuf[slot][:, kc, which * 128:(which + 1) * 128],
                    halo(kc), kc == 0, kc == 7, [("wbufB", slot, which), ("wbuf", slot), ("hT", hs, kc)], [("ps", puh)])
        u, ku = self.t16()
        self.copy("act", u[:, 1:TW + 1], self.ps[pu][:, :], [("ps", pu)], [(ku, 0)])
        self.copy("dve", u[:, 0:TW + 2:TW + 1], self.ps[puh][:, 0:2], [("ps", puh)], [(ku, 1)])
        st["ust"][i] = (u, ku)

    def sweepB2_tile(self, l, ti, last):
        sc, nc, cs = self.sc, self.nc, self.consts
        t0 = ti * TW
        tcols = slice(t0, t0 + TW)
        hs = ti % 2
        if ("b2", l, ti) not in self.pref:
            self.b2_loads(l, ti)
        halo = lambda kc: self.hT[hs][:, kc, 0:TW + 2:TW + 1]
        P8 = (0, 1, 2, 3, 4, 5, 6, 7)
        items = [(m, w) for m in range(22) for w in range(2)]
        st = self.b2st.setdefault((l, ti), {"ust": {}, "pcs": {}, "pre": 0})
        ust, pcs = st["ust"], st["pcs"]
        PRE = 3

        def S2(i):
            m, which = items[i]
            chunk = m + 22 * which
            u, ku = ust.pop(i)
            wc = lambda k: self.vecs[:, l, V_FDW + k * 44 + chunk:V_FDW + k * 44 + chunk + 1]
            t, kt = self.t32()
            if CONV_ON == "pe":
                pc_ = self.bank(P8)
                for k in range(3):
                    dg, kdg = self.diag_tile(l, V_FDW + k * 44 + chunk)
                    self.mm(self.ps[pc_][:, :], dg[:], u[:, k:k + TW], k == 0, k == 2,
                            [kdg, (ku, 0), (ku, 1)], [("ps", pc_)])
                pcs[i] = (self.ps[pc_][:, :], ("ps", pc_))
            else:
                self.act(t[:], u[:, 0:TW], AF.Identity, [ku, "vecs"], [kt], scale=wc(0))
                self.stt(t[:], u[:, 1:TW + 1], wc(1), t[:], ALU.mult, ALU.add, [ku, kt, "vecs"], [kt])
                self.stt(t[:], u[:, 2:TW + 2], wc(2), t[:], ALU.mult, ALU.add, [ku, kt, "vecs"], [kt])
                pcs[i] = (t[:], kt)

        def S3(m):
            (a0, k0), (a1, k1) = pcs.pop(2 * m), pcs.pop(2 * m + 1)
            if CONV_ON == "pe":
                sg, ksg = self.t32()
            else:
                sg, ksg = a0, k0
            self.act(sg if CONV_ON != "pe" else sg[:], a0, AF.Silu, [k0], [ksg])
            self.tt("dve", self.actb[:, m, :], sg if CONV_ON != "pe" else sg[:], a1, ALU.mult, [ksg, k1, "kvtok"], [("actb", m)])

        def w_dn_load(mo):
            sc.dma("sp", self.wdn[mo % 2], self.w_dn_s[l, :, mo, :, :], reads=self.wk[l]["dn"] + ["kvtok"],
                   writes=[("wdn", mo % 2)], key=("wdn", mo % 2))

        if ti > 0:
            w_dn_load(0)
            w_dn_load(1)
        for i in range(len(items) + 1):
            if ti == 0 and i == 6:
                w_dn_load(0)
                w_dn_load(1)
            if i < len(items) and i >= st["pre"]:
                self.b2_S1(l, ti, i, P8)
            if i >= 1:
                S2(i - 1)
                if (i - 1) % 2 == 1:
                    S3((i - 1) // 2)
        sc.dma("sp", self.h32[:, :, :], self.h1_32[:, tcols].rearrange("(c p) t -> p c t", p=128),
               reads=[("h1_32", ti, c) for c in range(8)], writes=[("h32", c) for c in range(8)], key="h32ld")
        if ti + 1 < self.NTO:
            if self.pair and ti + 1 == self.NTO - 1:
                self.exchange_h1_recv(l)
            self.b2_loads(l, ti + 1)
            for m_ in range(3):
                self.w_up_load(l, ti + 1, m_)
        for mo in range(8):
            slot = mo % 2
            if mo >= 2:
                w_dn_load(mo)
            pd = self.bank()
            for m in range(22):
                self.mm(self.ps[pd][:, :], self.wdn[slot][:, m, :], self.actb[:, m, :], m == 0, m == 21,
                        [("wdn", slot), ("actb", m)], [("ps", pd)])
            self.stt(self.big32[:, mo, :], self.h32[:, mo, :], ALPHA, self.ps[pd][:, :], ALU.mult, ALU.add,
                     [("h32", mo), ("ps", pd)], [("big32", mo)])
        def early():
            if ti + 1 < self.NTO and _os.environ.get("B2_EARLY", "1") == "1":
                for i_ in range(PRE):
                    self.b2_S1(l, ti + 1, i_, (0, 1, 2, 3, 4, 5))
                self.b2st[(l, ti + 1)]["pre"] = PRE

        self.layernorm8(l, V_L2G, V_L2B, mid=early)
        if last or self.pair:
            flat = self.big32[:].rearrange("p a b -> p (a b)")
            yflat = self.ybuf[:].rearrange("p a b -> p (a b)")
            for b in range(4):
                ot = flat[:, b * 1024:(b + 1) * 1024] if last else yflat[:, b * 1024:(b + 1) * 1024]
                okey = "big32" if last else "ybuf"
                for g in range(2):
                    pb = self.bank()
                    for j in range(4):
                        c = g * 4 + j
                        sc.op("pe", lambda c=c, j=j, pb=pb, b=b: nc.tensor.transpose(
                            out=self.ps[pb][:, j * 128:(j + 1) * 128], in_=self.h32[:, c, b * 128:(b + 1) * 128],
                            identity=cs[:, C_ID:C_ID + 128]), reads=[("h32", c), "consts"], writes=[("ps", pb)])
                    self.copy("act" if g == 0 else "dve", ot[:, g * 512:(g + 1) * 512], self.ps[pb][:, :],
                              [("ps", pb)], [(okey, 2 * b + g)])
                dst = self.out[t0 + b * 128:t0 + (b + 1) * 128, :] if last else self.xch_in[ti][b * 128:(b + 1) * 128, :]
                sc.dma(STQ, dst, ot,
                       reads=[(okey, 2 * b), (okey, 2 * b + 1)], writes=[("out" if last else "xch_in", ti, b)], key="outst")
            if not last:
                RG = self.RG
                sc.op("pool", lambda: nc.gpsimd.collective_compute("AllGather", ALU.bypass, replica_groups=RG,
                                                                   ins=[self.xch_in[ti].ap()], outs=[self.xch_out[ti].ap()]),
                      reads=[("xch_in", ti, b) for b in range(4)], writes=[("xch_out", ti)], dma_key=("cc_e2", ti), cc=True)
        if not last:
            st16 = self.big32[:].rearrange("p a b -> p (a b)")[:, 0:2048].bitcast(BF16).rearrange("p (c t) -> p c t", c=8)
            for m in range(8):
                self.copy("dve", st16[:, m, :], self.h32[:, m, :], [("h32", m)], [("big32", m // 2)])
            sc.dma(STQ, self.hin16[:, tcols].rearrange("(c p) t -> p c t", p=128), st16,
                   reads=[("big32", c) for c in range(4)],
                   writes=[("hin16", ti, bb, m) for bb in range(4) for m in range(8)], key="hin16st")
            sc.dma(STQ, self.hin32[:, tcols].rearrange("(c p) t -> p c t", p=128), self.h32[:, :, :],
                   reads=[("h32", m) for m in range(8)],
                   writes=[("hin32", ti, bb, m) for bb in range(4) for m in range(8)], key="hin32st")

    def exchange_h1_halo(self, l):
        sc, nc, cs, T = self.sc, self.nc, self.consts, self.T
        sm = self.small
        self.copy("dve", sm[:, 0:8], self.h32[:, :, TW - 1], [("h32", m) for m in range(8)], ["small"])
        sc.dma(STQ, self.e1_in.ap(), sm[:, 0:8], reads=["small"], writes=["e1_in"], key="e1st")
        RG = self.RG
        sc.op("pool", lambda: nc.gpsimd.collective_compute("AllGather", ALU.bypass, replica_groups=RG,
                                                           ins=[self.e1_in.ap()], outs=[self.e1_out.ap()]),
              reads=["e1_in"], writes=["e1_out"], dma_key="cc_e1", cc=True)

    def exchange_h1_recv(self, l):
        sc, nc, cs, T = self.sc, self.nc, self.consts, self.T
        sm = self.small
        sc.dma("sp", sm[:, 0:8], self.e1_out[0:128, :], reads=["e1_out"], writes=["small"], key="e1ld")
        sc.dma("sp", sm[:, 8:16], self.e1_out[128:256, :], reads=["e1_out"], writes=["small2"], key="e1ld")
        t, kt = self.t32()
        self.ts("dve", t[:, 0:8], sm[:, 0:8], cs[:, C_SELW:C_SELW + 1], ALU.mult, ["small", "consts"], [kt])
        o16, ko = self.t16()
        self.stt(o16[:, 0:8], sm[:, 8:16], cs[:, C_SELW + 1:C_SELW + 2], t[:, 0:8], ALU.mult, ALU.add,
                 ["small2", kt, "consts"], [ko])
        dst = self.h1_16[:, T + 1:T + 2].rearrange("(c p) o -> p (c o)", p=128)
        sc.dma(STQ, dst, o16[:, 0:8], reads=[ko], writes=["h1padR"], key=("sto", ko), slow=True)

    def exchange_stream(self, l):
        sc, nc, T = self.sc, self.nc, self.T
        yflat = self.ybuf[:].rearrange("p a b -> p (a b)")
        b16 = self.big32[:].rearrange("p a b -> p (a b)")[:, 0:1536].bitcast(BF16)
        sets = [(yflat[:, 0:2048], [[("ybuf", 0), ("ybuf", 1)], [("ybuf", 2), ("ybuf", 3)]],
                 self.ybuf[:, 4:6, :], [("ybuf", 4), ("ybuf", 5)]),
                (b16[:, 0:2048], [[("big32", 0)], [("big32", 1)]],
                 b16[:, 2048:3072].rearrange("p (g t) -> p g t", g=2), [("big32", 2), ("big32", 2)])]
        order = sorted(range(T // 128), key=lambda b_: ((T - 128 * (b_ + 1)) // TW, b_))
        for n_, bp in enumerate(order):
            ldv, ldk, outv, outk = sets[n_ % 2]
            r0 = T - 128 * (bp + 1)
            tj, rr = r0 // TW, r0 % TW
            for s_ in range(2):
                sc.dma("sp", ldv[:, s_ * 1024:(s_ + 1) * 1024], self.xch_out[tj][s_ * TW + rr:s_ * TW + rr + 128, :],
                       reads=[("xch_out", tj)], writes=ldk[s_], key=("x2ld", n_ % 2, s_))
            for g in range(2):
                pb = self.bank()
                for j in range(4):
                    c = g * 4 + j
                    for s_ in range(2):
                        self.mm(self.ps[pb][:, j * 128:(j + 1) * 128], ldv[:, s_ * 1024 + c * 128:s_ * 1024 + (c + 1) * 128],
                                self.Jsel[:, s_, :], s_ == 0, s_ == 1, ldk[s_] + ["Jsel"], [("ps", pb)])
                self.copy("act" if g == 0 else "dve", outv[:, g, :], self.ps[pb][:, :], [("ps", pb)], [outk[g]])
            col0 = T + bp * 128
            ti_, bb = col0 // TW, (col0 % TW) // 128
            for c in range(8):
                sc.dma(STQ, self.hin16[c * 128:(c + 1) * 128, col0:col0 + 128],
                       outv[:, c // 4, (c % 4) * 128:(c % 4 + 1) * 128],
                       reads=[outk[c // 4]], writes=[("hin16", ti_, bb, c)], key=("x2st", n_ % 2, c % 2))

    def build(self, n_layers=None, stop=None):
        L = self.L if n_layers is None else n_layers
        self.setup()
        LATE = L > 1 and _os.environ.get("LATE_CAST", "1") == "1"

        def casts():
            for l_ in range(1 if LATE else L):
                yield from self.cast_weights(l_)

        late_gen = self.cast_weights(1, late=True) if LATE else None

        def prep():
            yield from self.phase0()
            yield from self.rope_tables()

        self.merge(casts(), prep())
        for l in range(L):
            if stop in ("SETUP", "P0", "P0a"):
                break
            self.sc.fence()
            self.layer_weights_small(l)
            self.memset("dve", self.Sst[:], 0.0, ["Sst"])
            self.memset("pool", self.Sbf[:], 0.0, ["Sbf"])
            for ti in reversed(range(self.NT)):
                self.sweepA_tile(l, ti, late_gen if l == 0 else None)
            if l == 0 and late_gen is not None:
                for _ in late_gen:
                    pass
            if stop == "A":
                break
            self.memset("dve", self.Sst[:], 0.0, ["Sst"])
            self.memset("pool", self.Sbf[:], 0.0, ["Sbf"])
            for ti in range(self.NTO):
                self.sweepB1_tile(l, ti)
            if stop == "B1":
                break
            self.memset("dve", self.tok[:], 0.0, ["kvtok"])
            self.sc.ops[-1]["deps"].update({j: "war" for j in self.sc.readers.get("kvrd", ())})
            if self.pair:
                self.exchange_h1_halo(l)
                if self.NTO == 1:
                    self.exchange_h1_recv(l)
            for ti in range(self.NTO):
                self.sweepB2_tile(l, ti, last=(l == L - 1))
            if self.pair and l < L - 1:
                self.exchange_stream(l)
        st = self.sc.emit()
        return self.nc, st


def kchunk(w):
    K, N = w.shape
    return np.ascontiguousarray(w.reshape(K // 128, 128, N).transpose(1, 0, 2))


def pvec(v):
    return np.ascontiguousarray(np.asarray(v).reshape(-1, 128).T)


def make_consts(rev, ln_in_g, ln_in_b, slot):
    c = np.zeros((128, NCONST), np.float32)
    m = np.zeros((128, NMASK), np.float32)
    c[:, C_ID:C_ID + 128] = np.eye(128, dtype=np.float32)
    s = np.arange(128)[:, None]
    t = np.arange(128)[None, :]
    c[:, C_TRIU:C_TRIU + 128] = np.where(s <= t, -1.0 / 16.0, 0.0)
    c[:, C_TRID:C_TRID + 128] = np.where(s >= t, -1.0 / 16.0, 0.0)
    c[:, C_ONES:C_ONES + 128] = 1.0
    if not rev:
        m[:, M_MU:M_MU + 128] = (s <= t)
        m[:, M_MD:M_MD + 128] = (s > t)
    else:
        m[:, M_MU:M_MU + 128] = (s < t)
        m[:, M_MD:M_MD + 128] = (s >= t)
    hk = np.arange(128)[:, None] // 32
    hv = np.arange(256)[None, :] // 64
    m[:, M_BM:M_BM + 256] = (hk == hv)
    m[:, M_CME:M_CME + 256] = (hv % 2 == 0)
    m[:, M_CMO:M_CMO + 256] = (hv % 2 == 1)
    for i in range(128):
        m[i, M_J + 127 - i] = 1.0
    for i in range(64):
        m[i, M_SEL + i] = 1.0
        m[i, M_SEL + 128 + 64 + i] = 1.0
    for h in range(4):
        c[:, C_HM + h] = (np.arange(128) // 32 == h)
    inv = (10000.0 ** (-np.arange(0, 32, 2, dtype=np.float32) / 32)).astype(np.float32)
    c[64:80, C_ROPE] = inv
    c[80:96, C_ROPE] = inv
    c[64:80, C_ROPE + 1] = -1.0
    c[80:96, C_ROPE + 1] = 1.0
    c[:, C_LNIN:C_LNIN + 8] = pvec(ln_in_g)
    c[:, C_LNIN + 8:C_LNIN + 16] = pvec(ln_in_b)
    c[:, C_SELW + slot] = 1.0
    return c, m


def prep_layer_weights(inp, rev):
    L = inp["w_in"].shape[0]
    w_in_e, w_uq_e, w_ukv_e, w_a_e, vecs = [], [], [], [], []
    for l in range(L):
        w = np.asarray(inp["w_in"][l])
        sc_b, sc_c, sc_h = w[:, 0:256], w[:, 256:512], w[:, 512:768]
        gq, gk, gv, gg = w[:, 768:896], w[:, 896:1024], w[:, 1024:1280], w[:, 1280:1536]
        ga_f, ga_b = w[:, 1536:1552], w[:, 1552:1568]
        cq, ckv, kr = w[:, 1568:1824], w[:, 1824:1952], w[:, 1952:1984]
        ca, cg = w[:, 1984:2240], w[:, 2240:2496]
        z64 = np.zeros((1024, 64), np.float32)
        kr_sw = np.concatenate([kr[:, 16:32], kr[:, 0:16]], axis=1)
        ga_u, ga_d = (ga_f, ga_b) if not rev else (ga_b, ga_f)
        ext = np.concatenate([gq, gk, gv, ga_u, ga_d, ckv, z64, kr, z64, kr_sw, sc_c, sc_h, ca, cg, sc_b, gg, cq], axis=1)
        assert ext.shape[1] == NEXT
        w_in_e.append(kchunk(ext))
        wq = np.asarray(inp["w_mla_uq"][l])
        norm, swp = [], []
        for h in range(4):
            blk = wq[:, h * 96:(h + 1) * 96]
            norm.append(blk)
            swp.append(np.concatenate([blk[:, 0:64], blk[:, 80:96], blk[:, 64:80]], axis=1))
        w_uq_e.append(kchunk(np.concatenate(norm + swp, axis=1)))
        wkv = np.asarray(inp["w_mla_ukv"][l])
        kp = [wkv[:, h * 128:h * 128 + 64] for h in range(4)]
        vp = [wkv[:, h * 128 + 64:h * 128 + 128] for h in range(4)]
        w_ukv_e.append(np.ascontiguousarray(np.concatenate(kp + vp, axis=1)))
        wa = np.asarray(inp["w_gla_a_up"][l])
        ba = np.asarray(inp["b_gla_a"][l])
        iu, idn = (0, 1) if not rev else (1, 0)
        wae = np.zeros((33, 256), np.float32)
        wae[0:16, 0:128] = wa[iu]
        wae[16:32, 128:256] = wa[idn]
        wae[32, 0:128] = ba[iu]
        wae[32, 128:256] = ba[idn]
        w_a_e.append(wae)
        v = np.zeros((128, NV), np.float32)
        v[:, V_GH:V_GH + 2] = pvec(inp["g_gla_head"][l])
        v[:, V_GQ:V_GQ + 2] = pvec(inp["g_mla_q"][l])
        v[:, V_GKV:V_GKV + 1] = pvec(inp["g_mla_kv"][l])
        scw = np.asarray(inp["w_sc_conv"][l])
        cfw = np.asarray(inp["w_cfm_dw"][l])
        fdw = np.asarray(inp["w_ffn_dw"][l])
        if rev:
            scw, cfw, fdw = scw[::-1], cfw[::-1], fdw[::-1]
        for k in range(3):
            v[:, V_SCW + 2 * k:V_SCW + 2 * k + 2] = pvec(scw[k])
            v[:, V_FDW + 44 * k:V_FDW + 44 * k + 44] = pvec(fdw[k])
        for k in range(31):
            v[:, V_CFW + 2 * k:V_CFW + 2 * k + 2] = pvec(cfw[k])
        v[:, V_CFG:V_CFG + 2] = pvec(inp["g_cfm_ln"][l])
        v[:, V_CFB:V_CFB + 2] = pvec(inp["b_cfm_ln"][l])
        v[:, V_GBR:V_GBR + 8] = pvec(inp["g_branch"][l])
        v[:, V_L1G:V_L1G + 8] = pvec(inp["ln1_g"][l])
        v[:, V_L1B:V_L1B + 8] = pvec(inp["ln1_b"][l])
        v[:, V_L2G:V_L2G + 8] = pvec(inp["ln2_g"][l])
        v[:, V_L2B:V_L2B + 8] = pvec(inp["ln2_b"][l])
        vecs.append(v)
    return dict(w_in=np.stack(w_in_e), w_uq=np.stack(w_uq_e), w_ukv=np.stack(w_ukv_e), w_a=np.stack(w_a_e),
                vecs=np.stack(vecs),
                w_out=np.stack([kchunk(np.asarray(inp["w_out"][l])) for l in range(L)]),
                w_up=np.stack([kchunk(np.asarray(inp["w_ffn_up"][l])) for l in range(L)]),
                w_dn=np.stack([kchunk(np.asarray(inp["w_ffn_down"][l])) for l in range(L)]))


PAIR = True


def build_inputs(inputs, Bn, S, pair):
    x = np.asarray(inputs["x"], np.float32)
    pos = np.asarray(inputs["positions"], np.int32)
    in_maps = []
    if not pair:
        wf = prep_layer_weights(inputs, False)
        cst, msk = make_consts(False, inputs["ln_in_g"], inputs["ln_in_b"], 0)
        for c in range(8):
            b = c % Bn
            m = dict(wf)
            m.update(x=np.ascontiguousarray(x[b]), pos=np.ascontiguousarray(pos[b][None, :]), consts=cst, masks=msk)
            in_maps.append(m)
        return in_maps
    wfs = [prep_layer_weights(inputs, False), prep_layer_weights(inputs, True)]
    for c in range(2 * Bn):
        b, r = c // 2, c % 2
        m = dict(wfs[r])
        xs, ps = (x[b], pos[b]) if r == 0 else (x[b][::-1], pos[b][::-1])
        cst, msk = make_consts(r == 1, inputs["ln_in_g"], inputs["ln_in_b"], 1 - r)
        m.update(x=np.ascontiguousarray(xs), pos=np.ascontiguousarray(ps[None, :]), consts=cst, masks=msk)
        in_maps.append(m)
    return in_maps


def assemble(results, Bn, S, pair):
    if not pair:
        return np.stack([np.asarray(results[b]["out"]) for b in range(Bn)], axis=0).astype(np.float32)
    T = S // 2
    out = np.zeros((Bn, S, D), np.float32)
    for b in range(Bn):
        out[b, 0:T] = np.asarray(results[2 * b]["out"])
        out[b, T:S] = np.asarray(results[2 * b + 1]["out"])[::-1]
    return out


def kernel(**inputs):
    Bn, S, _ = inputs["x"].shape
    L = inputs["w_in"].shape[0]
    pair = PAIR and (2 * Bn <= 8)
    bld = Builder(S, S // 2 if pair else S, L, pair=pair)
    nc, st = bld.build()
    in_maps = build_inputs(inputs, Bn, S, pair)
    res = run_bass_kernel_spmd(nc, in_maps, core_ids=list(range(len(in_maps))))
    return assemble(res.results, Bn, S, pair)
```

```python
import math
import numpy as np
import concourse.bass as bass
import concourse.mybir as mybir
from concourse.bass_utils import run_bass_kernel_spmd

F32 = mybir.dt.float32
BF16 = mybir.dt.bfloat16
I32 = mybir.dt.int32
AF = mybir.ActivationFunctionType
ALU = mybir.AluOpType

D = 1024
DFF = 2816
NEXT = 2656
G1, GA, G2, G3, G4, B1G, B2G = 0, 512, 544, 864, 1376, 1888, 2400
EPS = 1e-5
DEPTH = 2
ALPHA = (2.0 * DEPTH) ** 0.25
TW = 512
STRICT_WAR = False
import os as _os
STQ = _os.environ.get("STQ", "act")
CONV_ON = _os.environ.get("CONV_ON", "dve")

C_ID, C_TRIU, C_TRID, C_ONES = 0, 128, 256, 384
C_HM, C_ROPE, C_LNIN, C_SELW = 512, 516, 520, 536
NCONST = 544
M_MU, M_MD, M_BM, M_CME, M_CMO, M_SEL, M_J = 0, 128, 256, 512, 768, 1024, 1280
NMASK = 1408
V_GH, V_GQ, V_GKV, V_SCW, V_CFW, V_CFG, V_CFB, V_GBR, V_L1G, V_L1B, V_FDW, V_L2G, V_L2B = \
    0, 2, 4, 5, 11, 73, 75, 77, 85, 93, 101, 233, 241
NV = 256


class Sched:
    COMPUTE = ("pe", "act", "dve", "pool")

    def __init__(self, nc):
        self.nc = nc
        self.engs = {"pe": nc.tensor, "act": nc.scalar, "dve": nc.vector,
                     "pool": nc.gpsimd, "sp": nc.sync}
        self.ops = []
        self.last_w = {}
        self.readers = {}

    def op(self, eng, fn, reads=(), writes=(), dma_key=None, cc=False):
        reads = [k[0] if (isinstance(k, tuple) and isinstance(k[0], tuple)) else k for k in reads]
        writes = [k[0] if (isinstance(k, tuple) and isinstance(k[0], tuple)) else k for k in writes]
        deps = {}
        for k in reads:
            j = self.last_w.get(k)
            if j is not None:
                deps[j] = "raw"
            if isinstance(k, tuple) and k[0] == "ps":
                for j in self.readers.get(k, ()):
                    if self.ops[j]["eng"] != eng and j not in deps:
                        deps[j] = "psx"
        for k in writes:
            j = self.last_w.get(k)
            if j is not None and j not in deps:
                deps[j] = "waw"
            for j in self.readers.get(k, ()):
                if j not in deps:
                    deps[j] = "war"
        idx = len(self.ops)
        self.ops.append(dict(eng=eng, fn=fn, deps=deps, dma_key=dma_key, cc=cc))
        for k in reads:
            lst = self.readers.setdefault(k, [])
            if dma_key is None:
                lst[:] = [j for j in lst
                          if not (self.ops[j]["dma_key"] is None and self.ops[j]["eng"] == eng)]
            lst.append(idx)
        for k in writes:
            self.last_w[k] = idx
            self.readers[k] = []
        return idx

    def fence(self):
        self.ops.append(dict(eng=None, fn=None, deps={}, dma_key=None, cc=False))

    def dma(self, queue, out, in_, reads=(), writes=(), key=None, slow=False):
        eng = self.engs[queue]
        if slow:
            return self.op(queue, lambda: eng.dma_start(out=out, in_=in_, allow_slow_non_contiguous=True),
                           reads=reads, writes=writes, dma_key=key)
        return self.op(queue, lambda: eng.dma_start(out=out, in_=in_),
                       reads=reads, writes=writes, dma_key=key)

    def emit(self):
        nc, ops = self.nc, self.ops

        def skip(o, pj, kind):
            return (pj["dma_key"] is None and o["dma_key"] is None and pj["eng"] == o["eng"]
                    and (o["eng"] == "pe" or kind == "psx" or
                         (kind in ("war", "waw") and not (STRICT_WAR or o["eng"] == "pool"))))
        needed = [False] * len(ops)
        lastc = {}
        for i, o in enumerate(ops):
            if o["eng"] is None:
                for j in lastc.values():
                    needed[j] = True
                continue
            for j, kind in o["deps"].items():
                pj = ops[j]
                if pj["dma_key"] is None and not skip(o, pj, kind):
                    needed[j] = True
            if o["dma_key"] is None:
                lastc[o["eng"]] = i
        sems = {e: nc.alloc_semaphore(f"sem_{e}") for e in self.COMPUTE}
        dma_sems, dma_tot = {}, {}
        cnt = {e: 0 for e in self.COMPUTE}
        val = [None] * len(ops)
        seen = {e: {} for e in self.engs}
        pending = {e: {} for e in self.engs}
        for i, o in enumerate(ops):
            if o["eng"] is None:
                for e in self.engs:
                    for ce in self.COMPUTE:
                        if cnt[ce] > 0:
                            pending[e][("c", ce)] = cnt[ce]
                    for k, tot in dma_tot.items():
                        pending[e][("dma", k)] = tot
                continue
            e = o["eng"]
            eng = self.engs[e]
            want = dict(pending[e])
            pending[e] = {}
            for j, kind in o["deps"].items():
                pj = ops[j]
                if pj["dma_key"] is not None:
                    want[("dma", pj["dma_key"])] = dma_tot[pj["dma_key"]]
                elif not skip(o, pj, kind):
                    sk = ("c", pj["eng"])
                    want[sk] = max(want.get(sk, 0), val[j])
            for sk, v in want.items():
                if seen[e].get(sk, 0) >= v:
                    continue
                if sk == ("c", e):
                    if o["dma_key"] is None and v > cnt[e]:
                        continue
                eng.wait_ge(dma_sems[sk[1]] if sk[0] == "dma" else sems[sk[1]], v)
                seen[e][sk] = v
            ins = o["fn"]()
            if o["dma_key"] is not None:
                k = o["dma_key"]
                if k not in dma_sems:
                    dma_sems[k] = nc.alloc_semaphore(f"dsem_{len(dma_sems)}")
                    dma_tot[k] = 0
                inc = 1 if o["cc"] else 16
                dma_tot[k] += inc
                ins.then_inc(dma_sems[k], inc)
            elif needed[i]:
                cnt[e] += 1
                ins.then_inc(sems[e], 1)
                val[i] = cnt[e]
        sp = self.engs["sp"]
        for k, tot in dma_tot.items():
            sp.wait_ge(dma_sems[k], tot)
        return dict(n_ops=len(ops), n_dma_sems=len(dma_sems))


class Builder:
    def __init__(self, S, T, L, pair, debug=False):
        self.S, self.T, self.L, self.pair, self.debug = S, T, L, pair, debug
        self.NT, self.NTO = S // TW, T // TW
        nc = self.nc = bass.Bass("TRN2", target_bir_lowering=False)
        self.sc = Sched(nc)
        self._bank = 0
        self.wk = {}
        self.b2st = {}
        self.late_per_tile = 9
        self.pref = set()
        self.SCP = (0, 1, 2)
        self.GP = (3, 4, 5)
        self._ctr = {}
        self.alloc()

    def dram(self, name, shape, dt):
        kind = "ExternalOutput" if self.debug else "Internal"
        return self.nc.dram_tensor(name, list(shape), dt, kind=kind)

    def alloc(self):
        nc, S, T, L = self.nc, self.S, self.T, self.L
        di = lambda n, s, dt=F32: nc.dram_tensor(n, list(s), dt, kind="ExternalInput")
        self.x = di("x", [S, D])
        self.pos = di("pos", [1, S], I32)
        self.consts_d = di("consts", [128, NCONST])
        self.masks_d = di("masks", [128, NMASK])
        self.vecs_d = di("vecs", [L, 128, NV])
        self.w_in_d = di("w_in", [L, 128, 8, NEXT])
        self.w_uq_d = di("w_uq", [L, 128, 2, 768])
        self.w_ukv_d = di("w_ukv", [L, 128, 512])
        self.w_a_d = di("w_a", [L, 33, 256])
        self.w_out_d = di("w_out", [L, 128, 8, D])
        self.w_up_d = di("w_up", [L, 128, 8, 2 * DFF])
        self.w_dn_d = di("w_dn", [L, 128, 22, D])
        self.out = nc.dram_tensor("out", [T, D], F32, kind="ExternalOutput")
        it = lambda n, s, dt: nc.dram_tensor(n, list(s), dt, kind="Internal")
        self.w_in_s = it("w_in_s", [L, 128, 8, NEXT], BF16)
        self.w_uq_s = it("w_uq_s", [L, 128, 2, 768], BF16)
        self.w_ukv_s = it("w_ukv_s", [L, 128, 512], BF16)
        self.w_out_s = it("w_out_s", [L, 128, 8, D], BF16)
        self.w_up_s = it("w_up_s", [L, 128, 44, 8, 128], BF16)
        self.w_dn_s = it("w_dn_s", [L, 128, 8, 22, 128], BF16)
        self.hin16 = self.dram("hin16", [D, S], BF16)
        self.hin32 = self.dram("hin32", [D, T], F32)
        self.h1_16 = self.dram("h1_16", [D, T + 2], BF16)
        self.h1_32 = self.dram("h1_32", [D, T], F32)
        self.SCW = 1 + (T + TW if self.pair else S) + 1
        self.GLW = 15 + (T + TW if self.pair else S) + 15
        self.scch_d = self.dram("scch_d", [256, self.SCW], BF16)
        self.glu_d = self.dram("glu_d", [256, self.GLW], BF16)
        self.oD_d = self.dram("oD_d", [256, T], BF16)
        self.rope_d = self.dram("rope_d", [2, 32, S], BF16)
        if self.pair:
            self.xch_in = [it(f"xch_in{i}", [TW, D], BF16) for i in range(T // TW)]
            self.xch_out = [it(f"xch_out{i}", [2 * TW, D], BF16) for i in range(T // TW)]
            self.e1_in = it("e1_in", [128, 8], F32)
            self.e1_out = it("e1_out", [256, 8], F32)
        sb = lambda n, s, dt=F32: nc.alloc_sbuf_tensor(n, list(s), dt)
        self.consts = sb("consts_sb", [128, NCONST])
        self.vecs = sb("vecs_sb", [128, L, NV])
        self.ident_bf = sb("ident_bf", [128, 128], BF16)
        self.onesD = sb("onesD", [128, 128], BF16)
        self.ones256 = sb("ones256", [128, 128], BF16)
        self.ones128 = sb("ones128", [128, 128], BF16)
        self.bo64 = sb("bo64", [128, 128], BF16)
        self.maskU4 = sb("maskU4", [128, 512], BF16)
        self.maskD4 = sb("maskD4", [128, 512], BF16)
        self.bm = sb("bm", [128, 256], BF16)
        self.cme = sb("cme", [128, 256], BF16)
        self.cmo = sb("cmo", [128, 256], BF16)
        self.sel = sb("sel", [64, 256], BF16)
        if self.pair:
            self.Jsel = sb("Jsel", [128, 2, 128], BF16)
        kv_bytes = 4 * S * 2 + (S // 128) * 4 * 65 * 2
        b2_bytes = 22 * 512 * 2 + 2 * 22 * 128 * 2
        self.kvreg = sb("kvreg", [128, max(kv_bytes, b2_bytes) // 2], BF16)
        o = 0
        self.kc = self.kvreg[0:96, o:o + 4 * S].rearrange("p (h s) -> p h s", h=4)
        o += 4 * S
        self.vaug = self.kvreg[:, o:o + (S // 128) * 260].rearrange("p (b h d) -> p b h d", h=4, d=65)
        self.actb = self.kvreg[:, 0:22 * 512].rearrange("p (m t) -> p m t", m=22)
        o = 22 * 512
        self.wdn = [self.kvreg[:, o + i * 22 * 128:o + (i + 1) * 22 * 128].rearrange("p (m c) -> p m c", m=22)
                    for i in range(2)]
        self.hT = [sb(f"hT{i}", [128, 8, TW + 2], BF16) for i in range(2)]
        self.wbuf = [sb(f"wbuf{i}", [128, 8, 544], BF16) for i in range(3)]
        self.wuq = sb("wuq", [128, 2, 768], BF16)
        self.wukv = sb("wukv", [128, 512], BF16)
        self.wa = sb("wa", [33, 256])
        self.dpool = [sb(f"dg{i}", [128, 128], BF16) for i in range(6)]
        self.big32 = sb("big32", [128, 8, 512])
        self.h32 = sb("h32", [128, 8, 512])
        self.ybuf = sb("ybuf", [128, 8, 512], BF16)
        self.ga_sb = sb("ga_sb", [33, 512])
        self.q_sb = sb("q_sb", [128, 512])
        self.k_sb = sb("k_sb", [128, 512])
        self.long32 = [sb(f"long32_{i}", [128, 512]) for i in range(4)]
        self.yg = [sb(f"yg{i}", [128, 512]) for i in range(2)]
        self.mean_t = sb("mean_t", [128, 512])
        self.var_t = sb("var_t", [128, 512])
        self.cqn = [sb(f"cqn{i}", [128, 512], BF16) for i in range(2)]
        self.win16 = [sb(f"win16_{i}", [128, 544], BF16) for i in range(4)]
        self.ropet = sb("ropet", [96, 2, 512], BF16)
        self.yh = [self.win16[i][0:64, 0:512] for i in range(4)]
        self.tmp32 = [sb(f"tmp32_{i}", [128, 512]) for i in range(4)]
        self.g32 = [sb(f"g32_{i}", [128, 384]) for i in range(4)]
        self.tmp16 = [sb(f"tmp16_{i}", [128, 544], BF16) for i in range(6)]
        self.g16 = [sb(f"g16_{i}", [128, 512], BF16) for i in range(5)]
        self.Sst = sb("Sst", [128, 256])
        self.Sbf = sb("Sbf", [128, 256], BF16)
        self.oDt = sb("oDt", [128, 2, 512], BF16)
        self.qc = [sb(f"qc{i}", [96, 512], BF16) for i in range(2)]
        self.small = sb("small", [128, 16])
        self.tok = sb("tok", [128, 2])
        self.ps = [nc.alloc_psum_tensor(f"ps{i}", [128, 512], F32) for i in range(8)]

    def bank(self, pool=(0, 1, 2)):
        n = self._ctr.get(pool, 0)
        self._ctr[pool] = n + 1
        return pool[n % len(pool)]

    def _rot(self, name, lst):
        n = self._ctr.get(name, 0)
        self._ctr[name] = n + 1
        i = n % len(lst)
        return lst[i], (name, i)

    def t32(self):
        return self._rot("tmp32", self.tmp32)

    def t16(self):
        return self._rot("tmp16", self.tmp16)

    def gt32(self):
        return self._rot("g32", self.g32)

    def gt16(self):
        return self._rot("g16", self.g16)

    def mm(self, out, lhsT, rhs, start, stop, reads, writes):
        nc = self.nc
        self.sc.op("pe", lambda: nc.tensor.matmul(out, lhsT, rhs, start=start, stop=stop),
                   reads=reads, writes=writes)

    def act(self, out, in_, func, reads, writes, bias=None, scale=None):
        nc = self.nc
        kw = {}
        if bias is not None:
            kw["bias"] = bias
        if scale is not None:
            kw["scale"] = scale
        self.sc.op("act", lambda: nc.scalar.activation(out=out, in_=in_, func=func, **kw),
                   reads=reads, writes=writes)

    def tt(self, eng, out, in0, in1, op, reads, writes):
        e = self.sc.engs[eng]
        self.sc.op(eng, lambda: e.tensor_tensor(out=out, in0=in0, in1=in1, op=op),
                   reads=reads, writes=writes)

    def ts(self, eng, out, in0, s1, op0, reads, writes, s2=None, op1=None):
        e = self.sc.engs[eng]
        if op1 is None:
            self.sc.op(eng, lambda: e.tensor_scalar(out=out, in0=in0, scalar1=s1, scalar2=None, op0=op0),
                       reads=reads, writes=writes)
        else:
            self.sc.op(eng, lambda: e.tensor_scalar(out=out, in0=in0, scalar1=s1, scalar2=s2, op0=op0, op1=op1),
                       reads=reads, writes=writes)

    def stt(self, out, in0, scalar, in1, op0, op1, reads, writes):
        nc = self.nc
        self.sc.op("dve", lambda: nc.vector.scalar_tensor_tensor(out=out, in0=in0, scalar=scalar, in1=in1,
                                                                   op0=op0, op1=op1),
                   reads=reads, writes=writes)

    def copy(self, eng, out, in_, reads, writes):
        if eng == "act":
            nc = self.nc
            self.sc.op("act", lambda: nc.scalar.copy(out=out, in_=in_), reads=reads, writes=writes)
        else:
            e = self.sc.engs[eng]
            self.sc.op(eng, lambda: e.tensor_copy(out=out, in_=in_), reads=reads, writes=writes)

    def memset(self, eng, ap, v, writes):
        e = self.sc.engs[eng]
        self.sc.op(eng, lambda: e.memset(ap, v), writes=writes)

    def rstd(self, out, in_, reads, okey):
        self.act(out, in_, AF.Ln, reads, [okey], bias=EPS, scale=1.0)
        self.act(out, out, AF.Exp, [okey], [okey], scale=-0.5)

    def setup(self):
        sc, nc, L = self.sc, self.nc, self.L
        sc.dma("sp", self.consts[:], self.consts_d.ap(), writes=["consts"], key="consts")
        for l in range(L):
            sc.dma("sp", self.vecs[:, l, :], self.vecs_d[l], writes=["vecs"], key="vecs")
        mk = self.big32[:].rearrange("p a b -> p (a b)")[:, 0:NMASK]
        MK = [("big32", i) for i in range(3)]
        self.RG = [[0, 1], [2, 3], [4, 5], [6, 7]]
        sc.dma("sp", mk, self.masks_d.ap(), writes=MK, key="masks")
        cs = self.consts
        self.copy("dve", self.ident_bf[:], cs[:, C_ID:C_ID + 128], ["consts"], ["ident_bf"])
        self.memset("pool", self.onesD[:], 1.0 / 1024, ["onesD"])
        self.memset("pool", self.ones256[:], 1.0 / 256, ["ones256"])
        self.memset("pool", self.ones128[:], 1.0 / 128, ["ones128"])
        self.memset("pool", self.bo64[:], 0.0, ["bo64"])
        self.memset("pool", self.bo64[0:64, 0:64], 1.0 / 64, ["bo64"])
        self.memset("pool", self.bo64[64:128, 64:128], 1.0 / 64, ["bo64"])
        for h in range(4):
            self.copy("dve", self.maskU4[:, h * 128:(h + 1) * 128], mk[:, M_MU:M_MU + 128], MK, ["maskU4"])
            self.copy("dve", self.maskD4[:, h * 128:(h + 1) * 128], mk[:, M_MD:M_MD + 128], MK, ["maskD4"])
        self.copy("dve", self.bm[:], mk[:, M_BM:M_BM + 256], MK, ["bm"])
        self.copy("dve", self.cme[:], mk[:, M_CME:M_CME + 256], MK, ["cme"])
        self.copy("dve", self.cmo[:], mk[:, M_CMO:M_CMO + 256], MK, ["cmo"])
        self.copy("dve", self.sel[:], mk[0:64, M_SEL:M_SEL + 256], MK, ["sel"])
        if self.pair:
            for s_ in range(2):
                self.ts("dve", self.Jsel[:, s_, :], mk[:, M_J:M_J + 128], cs[:, C_SELW + s_:C_SELW + s_ + 1], ALU.mult,
                        MK + ["consts"], ["Jsel"])
        self.memset("pool", self.ga_sb[32:33, :], 1.0, ["ga_ones"])
        zt, zk = self.t16()
        self.memset("pool", zt[:], 0.0, [zk])
        for j in range(2):
            r = slice(j * 128, (j + 1) * 128)
            sc.dma("sp", self.scch_d[r, 0:1], zt[:, 0:1], reads=[zk], writes=["scch_pad"], key="zp", slow=True)
            sc.dma("sp", self.glu_d[r, 0:15], zt[:, 0:15], reads=[zk], writes=["glu_pad"], key="zp", slow=True)
            if not self.pair:
                sc.dma("sp", self.scch_d[r, self.SCW - 1:self.SCW], zt[:, 0:1], reads=[zk], writes=["scch_pad"], key="zp", slow=True)
                sc.dma("sp", self.glu_d[r, self.GLW - 15:self.GLW], zt[:, 0:15], reads=[zk], writes=["glu_pad"], key="zp", slow=True)
        for c in range(8):
            r = slice(c * 128, (c + 1) * 128)
            sc.dma("sp", self.h1_16[r, 0:1], zt[:, 0:1], reads=[zk], writes=["h1padL"], key="zp", slow=True)
            if not self.pair:
                sc.dma("sp", self.h1_16[r, self.T + 1:self.T + 2], zt[:, 0:1], reads=[zk], writes=["h1padR"], key="zp", slow=True)

    def cast_weights(self, l, late=False):
        sc = self.sc
        jobs, names = [], []
        MB = 2 if late else 4
        for nm, src, dst, kc, n in (("in", self.w_in_d, self.w_in_s, 8, NEXT), ("out", self.w_out_d, self.w_out_s, 8, D),
                                    ("uq", self.w_uq_d, self.w_uq_s, 2, 768)):
            for k in range(kc):
                step = n if (not late or n <= 2048) else n // 2
                for c0 in range(0, n, step):
                    jobs.append((src[l, :, k, c0:c0 + step], dst[l, :, k, c0:c0 + step], step, None, 0))
                    names.append(nm)
        jobs.append((self.w_ukv_d[l], self.w_ukv_s[l], 512, None, 0))
        names.append("ukv")
        for m0 in range(0, 44, MB):
            jobs.append((self.w_up_d[l, :, :, m0 * 128:(m0 + MB) * 128], self.w_up_s[l, :, m0:m0 + MB, :, :],
                         MB * 1024, "up", MB))
            names.append("up")
        for mo in range(8):
            for k0 in ((0, 11) if late else (0,)):
                nk = 11 if late else 22
                jobs.append((self.w_dn_d[l, :, k0:k0 + nk, mo * 128:(mo + 1) * 128], self.w_dn_s[l, :, mo, k0:k0 + nk, :],
                             nk * 128, "dn", nk))
                names.append("dn")
        engs = ["act", "act"] if late else ["act", "dve"]
        wk = self.wk.setdefault(l, {})
        if not late:
            in_st = []
            for q in range(min(3, self.kvreg.shape[1] // 8192)):
                in_st.append((self.kvreg[:, q * 8192:(q + 1) * 8192].bitcast(F32), [("kvst", q)]))
            out_st = [(self.wbuf[q][:].rearrange("p a b -> p (a b)"), [("wbuf", q)]) for q in range(3)]
            out_st.append((self.ybuf[:].rearrange("p a b -> p (a b)"), [("ybufst",)]))
        else:
            bf, hf = self.big32[:].rearrange("p a b -> p (a b)"), self.h32[:].rearrange("p a b -> p (a b)")
            yf = self.ybuf[:].rearrange("p a b -> p (a b)")
            in_st = [(bf[:, 0:2048], [("big32", c) for c in range(4)]), (bf[:, 2048:4096], [("big32", c) for c in range(4, 8)]),
                     (hf[:, 0:2048], [("h32", c) for c in range(4)]), (hf[:, 2048:4096], [("h32", c) for c in range(4, 8)])]
            out_st = [(yf[:, 0:2048], [("ybuf", c) for c in range(4)]), (yf[:, 2048:4096], [("ybuf", c) for c in range(4, 8)])]
        base = self._ctr.get("castjob", 0)
        self._ctr["castjob"] = base + len(jobs)
        tag = "L" if late else "E"

        def emit_load(i):
            s_ap, d_ap, n, blk, par = jobs[i]
            f32v, k32s = in_st[(base + i) % len(in_st)]
            dstv = f32v[:, 0:n]
            if blk == "up":
                dstv = dstv.rearrange("p (k c) -> p k c", k=8)
            elif blk == "dn":
                dstv = dstv.rearrange("p (k c) -> p k c", k=par)
            sc.dma(STQ if late else "sp", dstv, s_ap, writes=k32s, key=("st32" + tag, (base + i) % len(in_st)))

        def emit_cast_store(i):
            s_ap, d_ap, n, blk, par = jobs[i]
            gi = base + i
            f32v, k32s = in_st[gi % len(in_st)]
            b16v, k16 = out_st[gi % len(out_st)]
            v32, v16 = f32v[:, 0:n], b16v[:, 0:n]
            if blk == "up":
                self.copy(engs[gi % 2], v16.rearrange("p (m k c) -> p m k c", m=par, k=8),
                          v32.rearrange("p (k m c) -> p m k c", k=8, m=par), k32s, k16)
                src16 = v16.rearrange("p (m k c) -> p m k c", m=par, k=8)
            elif blk == "dn":
                self.copy(engs[gi % 2], v16, v32, k32s, k16)
                src16 = v16.rearrange("p (k c) -> p k c", k=par)
            else:
                self.copy(engs[gi % 2], v16, v32, k32s, k16)
                src16 = v16
            wk.setdefault(names[i], []).append(("wscr", l, i))
            sc.dma(STQ, d_ap, src16, reads=k16, writes=[("wscr", l, i)], key=("st16" + tag, gi % len(out_st)))

        LA = min(3, len(in_st) - 1)
        for j in range(len(jobs) + LA):
            if j < len(jobs):
                emit_load(j)
            if j >= LA:
                emit_cast_store(j - LA)
            yield

    def phase0(self):
        sc, nc, S, T = self.sc, self.nc, self.S, self.T
        cs = self.consts
        flat32 = self.big32[:].rearrange("p a b -> p (a b)")
        hflat = self.h32[:].rearrange("p a b -> p (a b)")
        sm = self.small
        import os
        nblk = int(os.environ.get("P0_BLOCKS", S // 128))
        nostore = os.environ.get("P0_NOSTORE") == "1"
        nocast = os.environ.get("P0_NOCAST") == "1"
        for b in range(nblk):
            xt = flat32[:, 0:1024]
            xn = flat32[:, 1024:2048]
            hb = hflat[:, 0:1024]
            sc.dma("sp", xt, self.x[b * 128:(b + 1) * 128, :], writes=[("big32", 0), ("big32", 1)], key="p0x")
            for i in range(2):
                sc.op("dve", lambda i=i: nc.vector.bn_stats(out=sm[:, 6 * i:6 * i + 6], in_=xt[:, i * 512:(i + 1) * 512]),
                      reads=[("big32", 0), ("big32", 1)], writes=[("p0st", i)])
            sc.op("dve", lambda: nc.vector.bn_aggr(out=sm[:, 12:14], in_=sm[:, 0:12]),
                  reads=[("p0st", 0), ("p0st", 1)], writes=["p0mv"])
            self.act(sm[:, 14:15], sm[:, 13:14], AF.Ln, ["p0mv"], ["p0rs"], bias=EPS, scale=1.0)
            self.act(sm[:, 14:15], sm[:, 14:15], AF.Exp, ["p0rs"], ["p0rs"], scale=-0.5)
            self.stt(sm[:, 15:16], sm[:, 12:13], -1.0, sm[:, 14:15], ALU.mult, ALU.mult, ["p0mv", "p0rs"], ["p0nb"])
            self.act(xn, xt, AF.Identity, [("big32", 0), ("big32", 1), "p0rs", "p0nb"], [("big32", 2), ("big32", 3)], bias=sm[:, 15:16], scale=sm[:, 14:15])
            for g in range(2):
                pb = self.bank()
                for j in range(4):
                    c = g * 4 + j
                    sc.op("pe", lambda c=c, j=j, pb=pb: nc.tensor.transpose(
                        out=self.ps[pb][:, j * 128:(j + 1) * 128], in_=xn[:, c * 128:(c + 1) * 128],
                        identity=cs[:, C_ID:C_ID + 128]), reads=[("big32", 2), ("big32", 3), "consts"], writes=[("ps", pb)])
                for j in range(4):
                    c = g * 4 + j
                    self.act(hb[:, c * 128:(c + 1) * 128], self.ps[pb][:, j * 128:(j + 1) * 128], AF.Identity,
                             [("ps", pb), "consts"], [("h32", c // 4)],
                             bias=cs[:, C_LNIN + 8 + c:C_LNIN + 9 + c], scale=cs[:, C_LNIN + c:C_LNIN + c + 1])
            if nocast:
                continue
            h16 = [self.t16(), self.t16()]
            for g in range(2):
                self.copy("dve" if g else "act", h16[g][0][:, 0:512], hb[:, g * 512:(g + 1) * 512],
                          [("h32", g)], [h16[g][1]])
            yield
            for c in range(8):
                if nostore:
                    break
                src16 = h16[c // 4][0][:, (c % 4) * 128:(c % 4 + 1) * 128]
                sc.dma(STQ, self.hin16[c * 128:(c + 1) * 128, b * 128:(b + 1) * 128], src16,
                       reads=[h16[c // 4][1]], writes=[("hin16", b // 4, b % 4, c)], key=("sto", h16[c // 4][1]))
                if b * 128 < T:
                    sc.dma(STQ, self.hin32[c * 128:(c + 1) * 128, b * 128:(b + 1) * 128], hb[:, c * 128:(c + 1) * 128],
                           reads=[("h32", c // 4)], writes=[("hin32", b // 4, b % 4, c)], key="p0o32")

    def rope_tables(self):
        sc, nc, S = self.sc, self.nc, self.S
        cs = self.consts
        R = slice(64, 96)
        inv = cs[R, C_ROPE:C_ROPE + 1]
        sgn = cs[R, C_ROPE + 1:C_ROPE + 2]
        TWO_PI = 2.0 * math.pi
        C1 = 6.28125
        C2 = TWO_PI - C1
        ang = self.mean_t
        for b in range(S // 512):
            posi, kpi = self.t32()
            src = bass.AP(self.pos, b * 512, [[0, 32], [1, 512]])
            posv = posi[R, :].bitcast(I32)
            sc.dma("sp", posv, src, writes=[kpi], key="pos")
            self.copy("dve", ang[R, :], posv, [kpi], ["mean_t"])
            self.ts("dve", ang[R, :], ang[R, :], inv, ALU.mult, ["mean_t", "consts"], ["mean_t"])
            for ti_, (phase, scale) in enumerate(((math.pi / 2, None), (0.0, sgn))):
                a2, k2 = self.t32()
                qi, kq = self.t32()
                qf, kf = self.t32()
                self.ts("dve", a2[R, :], ang[R, :], phase, ALU.add, ["mean_t"], [k2])
                qiv = qi[R, :].bitcast(I32)
                self.ts("dve", qiv, a2[R, :], 1.0 / TWO_PI, ALU.mult, [k2], [kq])
                self.copy("dve", qf[R, :], qiv, [kq], [kf])
                self.stt(a2[R, :], qf[R, :], -C1, a2[R, :], ALU.mult, ALU.add, [kf, k2], [k2])
                self.stt(a2[R, :], qf[R, :], -C2, a2[R, :], ALU.mult, ALU.add, [kf, k2], [k2])
                self.ts("dve", qf[R, :], a2[R, :], math.pi, ALU.is_gt, [k2], [kf])
                self.stt(a2[R, :], qf[R, :], -TWO_PI, a2[R, :], ALU.mult, ALU.add, [kf, k2], [k2])
                self.ts("dve", qf[R, :], a2[R, :], -math.pi, ALU.is_lt, [k2], [kf])
                self.stt(a2[R, :], qf[R, :], TWO_PI, a2[R, :], ALU.mult, ALU.add, [kf, k2], [k2])
                self.ts("dve", a2[R, :], a2[R, :], math.pi, ALU.min, [k2], [k2], s2=-math.pi, op1=ALU.max)
                o16, ko = self.t16()
                if scale is None:
                    self.act(o16[R, 0:512], a2[R, :], AF.Sin, [k2], [ko])
                else:
                    self.act(o16[R, 0:512], a2[R, :], AF.Sin, [k2, "consts"], [ko], scale=scale)
                sc.dma(STQ, self.rope_d[ti_, :, b * 512:(b + 1) * 512], o16[R, 0:512], reads=[ko],
                       writes=[("rope", b, ti_)], key=("sto", ko))
            yield

    def load_rope(self, ti):
        for i in range(2):
            self.sc.dma("sp", self.ropet[64:96, i, :], self.rope_d[i, :, ti * TW:(ti + 1) * TW],
                        reads=[("rope", ti, i)], writes=[("ropet", i)], key=("ropet", i))

    def load_hT(self, ti, slot):
        t0 = ti * TW
        self.sc.dma("sp", self.hT[slot][:, :, 0:TW], self.hin16[:, t0:t0 + TW].rearrange("(c p) t -> p c t", p=128),
                    reads=[("hin16", ti, bb, kc) for bb in range(4) for kc in range(8)],
                    writes=[("hT", slot, kc) for kc in range(8)], key=("hT", slot))

    def load_w(self, scr, l, c0, n, slot):
        self.sc.dma("sp", self.wbuf[slot][:, :, 0:n], scr[l, :, :, c0:c0 + n],
                    reads=self.wk[l]["in" if scr is self.w_in_s else "out"],
                    writes=[("wbuf", slot)], key=("wbuf", slot))

    def proj_fm(self, pb, M, slot, c0, hslot):
        for kc in range(8):
            self.mm(self.ps[pb][0:M, :], self.wbuf[slot][:, kc, c0:c0 + M], self.hT[hslot][:, kc, 0:TW],
                    kc == 0, kc == 7, [("wbuf", slot), ("hT", hslot, kc)], [("ps", pb)])

    def diag_tile(self, l, vcol):
        dg, kdg = self._rot("dg", self.dpool)
        if kdg[1] % 2 == 0:
            self.ts("dve", dg[:], self.ident_bf[:], self.vecs[:, l, vcol:vcol + 1], ALU.mult, ["ident_bf", "vecs"], [kdg])
        else:
            self.act(dg[:], self.ident_bf[:], AF.Identity, ["ident_bf", "vecs"], [kdg], scale=self.vecs[:, l, vcol:vcol + 1])
        return dg, kdg

    def merge(self, *gens):
        gens = list(gens)
        while gens:
            for g_ in list(gens):
                try:
                    next(g_)
                except StopIteration:
                    gens.remove(g_)

    def gla_chunk(self, X, sub, with_out, hslot, wslot):
        GP = self.GP
        cs = self.consts
        cols = slice(sub * 128, (sub + 1) * 128)
        c_tri = C_TRIU if X == 0 else C_TRID
        tri = cs[:, c_tri:c_tri + 128]
        mask4 = self.maskU4 if X == 0 else self.maskD4
        mkey = "maskU4" if X == 0 else "maskD4"
        pkv = self.bank(GP)
        for kc in range(8):
            self.mm(self.ps[pkv][:, 0:384], self.hT[hslot][:, kc, cols], self.wbuf[wslot][:, kc, 128:512],
                    kc == 0, kc == 7, [("hT", hslot, kc), ("wbuf", wslot)], [("ps", pkv)])
        pla = self.bank(GP)
        self.mm(self.ps[pla][:, 0:128], self.ga_sb[0:33, cols], self.wa[0:33, X * 128:(X + 1) * 128], True, True,
                ["ga_sb", "ga_ones", "wa"], [("ps", pla)])
        yield
        lp, klp = self.gt32()
        self.act(lp[:, 0:128], self.ps[pla][:, 0:128], AF.Exp, [("ps", pla)], [klp], scale=-1.0)
        self.act(lp[:, 0:128], lp[:, 0:128], AF.Ln, [klp], [klp], bias=1.0, scale=1.0)
        yield
        pB = self.bank(GP)
        self.mm(self.ps[pB][:, 0:128], tri, lp[:, 0:128], True, True, ["consts", klp], [("ps", pB)])
        self.mm(self.ps[pB][:, 128:256], lp[:, 0:128], tri, True, True, ["consts", klp], [("ps", pB)])
        yield
        E, kE = self.gt32()
        self.act(E[:, 0:128], self.ps[pB][:, 0:128], AF.Exp, [("ps", pB)], [(kE, 0)], scale=-1.0)
        self.act(E[:, 128:256], self.ps[pB][:, 128:256], AF.Exp, [("ps", pB)], [(kE, 1)], scale=1.0)
        self.act(E[:, 256:384], self.ps[pB][:, 128:256], AF.Exp, [("ps", pB)], [(kE, 2)], scale=-1.0)
        yield
        V, kV = self.gt16()
        self.tt("dve", V[:, 0:256], self.ps[pkv][:, 128:384], self.cme[:], ALU.mult, [("ps", pkv), "cme"], [(kV, 0)])
        self.tt("dve", V[:, 256:512], self.ps[pkv][:, 128:384], self.cmo[:], ALU.mult, [("ps", pkv), "cmo"], [(kV, 1)])
        Kt, kKt = self.gt16()
        self.tt("dve", Kt[:, 0:128], self.ps[pkv][:, 0:128], E[:, 0:128], ALU.mult, [("ps", pkv), (kE, 0)], [kKt])
        po = None
        if with_out:
            Q, kQ = self.gt16()
            self.stt(Q[:, 0:128], self.q_sb[:, cols], 32.0 ** -0.5, E[:, 128:256], ALU.mult, ALU.mult,
                     ["q_sb", (kE, 1)], [kQ])
            Kh, kKh = self.gt16()
            for h in range(4):
                self.stt(Kh[:, h * 128:(h + 1) * 128], self.k_sb[:, cols], cs[:, C_HM + h:C_HM + h + 1], E[:, 256:384],
                         ALU.mult, ALU.mult, ["k_sb", (kE, 2), "consts"], [(kKh, h)])
            yield
            psc = self.bank(GP)
            for h in range(4):
                self.mm(self.ps[psc][:, h * 128:(h + 1) * 128], Kh[:, h * 128:(h + 1) * 128], Q[:, 0:128], True, True,
                        [(kKh, h), kQ], [("ps", psc)])
            yield
            A, kA = self.gt16()
            self.tt("dve", A[:, 0:512], self.ps[psc][:, 0:512], mask4[:], ALU.mult, [("ps", psc), mkey], [kA])
            yield
            po = self.bank(GP)
            for j in range(2):
                o = self.ps[po][:, j * 128:(j + 1) * 128]
                self.mm(o, V[:, j * 128:(j + 1) * 128], A[:, (2 * j) * 128:(2 * j + 1) * 128], True, False,
                        [(kV, 0), kA], [("ps", po)])
                self.mm(o, V[:, 256 + j * 128:256 + (j + 1) * 128], A[:, (2 * j + 1) * 128:(2 * j + 2) * 128], False, False,
                        [(kV, 1), kA], [("ps", po)])
                self.mm(o, self.Sbf[:, j * 128:(j + 1) * 128], Q[:, 0:128], False, True, ["Sbf", kQ], [("ps", po)])
        yield
        pdS = self.bank(GP)
        self.mm(self.ps[pdS][:, 0:256], Kt[:, 0:128], V[:, 0:256], True, False, [kKt, (kV, 0)], [("ps", pdS)])
        self.mm(self.ps[pdS][:, 0:256], Kt[:, 0:128], V[:, 256:512], False, True, [kKt, (kV, 1)], [("ps", pdS)])
        yield
        ecol = E[:, 255:256] if X == 0 else E[:, 128:129]
        S1, kS1 = self.gt32()
        self.stt(S1[:, 0:256], self.ps[pdS][:, 0:256], ecol, self.bm[:], ALU.mult, ALU.mult,
                 [("ps", pdS), (kE, 1), "bm"], [kS1])
        self.stt(self.Sst[:], self.Sst[:], ecol, S1[:, 0:256], ALU.mult, ALU.add, ["Sst", (kE, 1), kS1], ["Sst"])
        self.copy("act", self.Sbf[:], self.Sst[:], ["Sst"], ["Sbf"])
        self._po = po
        yield

    def sweepA_tile(self, l, ti, extra=None):
        sc, T = self.sc, self.T
        t0 = ti * TW
        own = t0 < T
        need_conv = t0 < (T + TW if self.pair else self.S)
        hs = ti % 2
        vec = self.vecs
        tcols = slice(t0, t0 + TW)
        R = slice(64, 96)
        self.load_hT(ti, hs)
        self.load_rope(ti)
        self.load_w(self.w_in_s, l, G2, 320, 0)
        gs = 1 + (ti % 2)
        ms = 3 - gs
        self.load_w(self.w_in_s, l, G1, 544, gs)
        pga = self.bank()
        self.proj_fm(pga, 32, gs, 512, hs)
        self.copy("act", self.ga_sb[0:32, :], self.ps[pga][0:32, :], [("ps", pga)], ["ga_sb"])
        if own:
            pq, pk2 = self.bank(), self.bank()
            self.proj_fm(pq, 128, gs, 0, hs)
            self.proj_fm(pk2, 128, gs, 128, hs)
            self.copy("act", self.q_sb[:], self.ps[pq][:, :], [("ps", pq)], ["q_sb"])
            self.copy("dve", self.k_sb[:], self.ps[pk2][:, :], [("ps", pk2)], ["k_sb"])

        def main():
            pck = self.bank()
            self.proj_fm(pck, 128, 0, 0, hs)
            sq, ksq = self.t16()
            ck, kck = self.t32()
            self.act(sq[:, 0:512], self.ps[pck][:, :], AF.Square, [("ps", pck)], [ksq])
            self.copy("dve", ck[:], self.ps[pck][:, :], [("ps", pck)], [kck])
            yield
            pms = self.bank()
            self.mm(self.ps[pms][:, :], self.ones128[:], sq[:, 0:512], True, True, ["ones128", ksq], [("ps", pms)])
            rs, krs = self.t32()
            self.rstd(rs[:], self.ps[pms][:, :], [("ps", pms)], krs)
            ckn, kckn = self.t16()
            self.stt(ckn[:, 0:512], ck[:], vec[:, l, V_GKV:V_GKV + 1], rs[:], ALU.mult, ALU.mult, [kck, krs, "vecs"], [kckn])
            yield
            for h in range(4):
                pk = self.bank()
                self.mm(self.ps[pk][0:64, :], self.wukv[:, h * 64:(h + 1) * 64], ckn[:, 0:512], True, True,
                        ["wukv", kckn], [("ps", pk)])
                self.copy("act" if h % 2 == 0 else "dve", self.kc[0:64, h, tcols], self.ps[pk][0:64, :],
                          [("ps", pk)], [("kc", ti, h)])
                yield
            for sub in range(4):
                pv = self.bank()
                self.mm(self.ps[pv][:, 0:256], ckn[:, sub * 128:(sub + 1) * 128], self.wukv[:, 256:512], True, True,
                        ["wukv", kckn], [("ps", pv)])
                blk = ti * 4 + sub
                self.copy("dve" if sub % 2 == 0 else "act", self.vaug[:, blk, :, 0:64],
                          self.ps[pv][:, 0:256].rearrange("p (h d) -> p h d", h=4), [("ps", pv)], [("vaug", ti)])
                yield
            self.memset("dve", self.vaug[:, ti * 4:ti * 4 + 4, :, 64:65], 1.0, [("vaug1", ti)])
            pka = self.bank()
            self.proj_fm(pka, 96, 0, 128, hs)
            r1, kr1 = self.t32()
            self.tt("dve", r1[R, :], self.ps[pka][R, :], self.ropet[R, 0, :], ALU.mult, [("ps", pka), ("ropet", 0)], [kr1])
            yield
            pkb = self.bank()
            self.proj_fm(pkb, 96, 0, 224, hs)
            r2, kr2 = self.t32()
            self.tt("dve", r2[R, :], self.ps[pkb][R, :], self.ropet[R, 1, :], ALU.mult, [("ps", pkb), ("ropet", 1)], [kr2])
            self.tt("dve", self.kc[R, 0, tcols], r1[R, :], r2[R, :], ALU.add, [kr1, kr2], [("kcr", ti, 0)])
            for h in range(1, 4):
                self.copy("act" if h % 2 else "dve", self.kc[R, h, tcols], self.kc[R, 0, tcols],
                          [("kcr", ti, 0)], [("kcr", ti, h)])
            yield
            if need_conv:
                self.load_w(self.w_in_s, l, G3, 512, ms)
                for j in range(2):
                    pc_, ph_ = self.bank(), self.bank()
                    self.proj_fm(pc_, 128, ms, j * 128, hs)
                    self.proj_fm(ph_, 128, ms, 256 + j * 128, hs)
                    tmp, ktmp = self.t32()
                    self.copy("act", tmp[:], self.ps[pc_][:, :], [("ps", pc_)], [ktmp])
                    o16, ko16 = self.t16()
                    self.tt("dve", o16[:, 0:512], tmp[:], self.ps[ph_][:, :], ALU.mult, [ktmp, ("ps", ph_)], [ko16])
                    sc.dma(STQ, self.scch_d[j * 128:(j + 1) * 128, 1 + t0:1 + t0 + TW], o16[:, 0:512],
                           reads=[ko16], writes=[("scch", ti, j)], key=("sto", ko16))
                    yield
                self.load_w(self.w_in_s, l, G4, 512, 0)
                for j in range(2):
                    pa_, pg_ = self.bank(), self.bank()
                    self.proj_fm(pa_, 128, 0, j * 128, hs)
                    self.proj_fm(pg_, 128, 0, 256 + j * 128, hs)
                    tmp, ktmp = self.t32()
                    self.act(tmp[:], self.ps[pg_][:, :], AF.Sigmoid, [("ps", pg_)], [ktmp])
                    o16, ko16 = self.t16()
                    self.tt("dve", o16[:, 0:512], tmp[:], self.ps[pa_][:, :], ALU.mult, [ktmp, ("ps", pa_)], [ko16])
                    sc.dma(STQ, self.glu_d[j * 128:(j + 1) * 128, 15 + t0:15 + t0 + TW], o16[:, 0:512],
                           reads=[ko16], writes=[("glu", ti, j)], key=("sto", ko16))
                    yield

        def side():
            for sub in (3, 2, 1, 0):
                yield from self.gla_chunk(1, sub, own, hs, gs)
                if own:
                    po = self._po
                    self.copy("act", self.oDt[:, :, sub * 128:(sub + 1) * 128],
                              self.ps[po][:, 0:256].rearrange("p (j t) -> p j t", j=2), [("ps", po)], [("oDt", 0), ("oDt", 1)])
            if own:
                for j in range(2):
                    sc.dma(STQ, self.oD_d[j * 128:(j + 1) * 128, tcols], self.oDt[:, j, :],
                           reads=[("oDt", j)], writes=[("oD", ti, j)], key="oDst")

        def extra_slice():
            for _ in range(self.late_per_tile):
                try:
                    next(extra)
                except StopIteration:
                    return
                yield

        if extra is None:
            self.merge(main(), side())
        else:
            self.merge(main(), side(), extra_slice())

    def layer_weights_small(self, l):
        sc = self.sc
        sc.dma("sp", self.wuq[:], self.w_uq_s[l], reads=self.wk[l]["uq"], writes=["wuq"], key="wuq")
        sc.dma("sp", self.wukv[:], self.w_ukv_s[l], reads=self.wk[l]["ukv"], writes=["wukv"], key="wukv")
        sc.dma("sp", self.wa[:], self.w_a_d[l], writes=["wa"], key="wa")

    def group_rms(self, l, g):
        pms = self.bank()
        for j in range(2):
            sq, ksq = self.t16()
            self.act(sq[:, 0:512], self.yg[j][:], AF.Square, [("yg", j)], [ksq])
            self.mm(self.ps[pms][:, :], self.ones256[:], sq[:, 0:512], j == 0, j == 1, ["ones256", ksq], [("ps", pms)])
        rs, krs = self.t32()
        self.rstd(rs[:], self.ps[pms][:, :], [("ps", pms)], krs)
        for j in range(2):
            c = V_GBR + 2 * g + j
            self.stt(self.ybuf[:, 2 * g + j, :], self.yg[j][:], self.vecs[:, l, c:c + 1], rs[:],
                     ALU.mult, ALU.mult, [("yg", j), krs, "vecs"], [("ybuf", 2 * g + j)])

    def layernorm8(self, l, gcol, bcol, mid=None):
        r = self.big32
        pm, pq = self.bank((6, 7)), self.bank((6, 7))
        for m in range(8):
            rb, krb = self.t16()
            rq, krq = self.t16()
            self.copy("dve" if m % 2 else "act", rb[:, 0:512], r[:, m, :], [("big32", m)], [krb])
            self.act(rq[:, 0:512], r[:, m, :], AF.Square, [("big32", m)], [krq])
            self.mm(self.ps[pm][:, :], self.onesD[:], rb[:, 0:512], m == 0, m == 7, ["onesD", krb], [("ps", pm)])
            self.mm(self.ps[pq][:, :], self.onesD[:], rq[:, 0:512], m == 0, m == 7, ["onesD", krq], [("ps", pq)])
        mean, var = self.mean_t, self.var_t
        msq, kmsq = self.t32()
        self.copy("dve", mean[:], self.ps[pm][:, :], [("ps", pm)], ["mean_t"])
        self.act(msq[:], self.ps[pm][:, :], AF.Square, [("ps", pm)], [kmsq])
        self.tt("dve", var[:], self.ps[pq][:, :], msq[:], ALU.subtract, [("ps", pq), kmsq], ["var_t"])
        self.rstd(var[:], var[:], ["var_t"], "var_t")
        if mid is not None:
            mid()
        for m in range(8):
            t, kt = self.t32()
            self.tt("dve", t[:], r[:, m, :], mean[:], ALU.subtract, [("big32", m), "mean_t"], [kt])
            self.stt(t[:], t[:], self.vecs[:, l, gcol + m:gcol + m + 1], var[:], ALU.mult, ALU.mult, [kt, "var_t", "vecs"], [kt])
            self.act(self.h32[:, m, :], t[:], AF.Identity, [kt, "vecs"], [("h32", m)],
                     bias=self.vecs[:, l, bcol + m:bcol + m + 1], scale=1.0)

    def mla_gen(self, l, ti, hs):
        sc, cs, S = self.sc, self.consts, self.S
        vec = self.vecs
        t0 = ti * TW
        tcols = slice(t0, t0 + TW)
        self.load_w(self.w_in_s, l, B2G, 256, 1)
        cq = self.long32[2:4]
        pms = self.bank((6, 7))
        for j in range(2):
            pc_ = self.bank()
            self.proj_fm(pc_, 128, 1, j * 128, hs)
            self.copy("dve", cq[j][:], self.ps[pc_][:, :], [("ps", pc_)], [("long32", 2 + j)])
            sq, ksq = self.t16()
            self.act(sq[:, 0:512], self.ps[pc_][:, :], AF.Square, [("ps", pc_)], [ksq])
            self.mm(self.ps[pms][:, :], self.ones256[:], sq[:, 0:512], j == 0, j == 1, ["ones256", ksq], [("ps", pms)])
        rs, krs = self.t32()
        self.rstd(rs[:], self.ps[pms][:, :], [("ps", pms)], krs)
        for j in range(2):
            self.stt(self.cqn[j][:], cq[j][:], vec[:, l, V_GQ + j:V_GQ + j + 1], rs[:], ALU.mult, ALU.mult,
                     [("long32", 2 + j), krs, "vecs"], [("cqn", j)])
        R = slice(64, 96)
        scale = 96.0 ** -0.5
        nblk = S // 128
        for h in range(4):
            qc = self.qc[h % 2]
            kqc = ("qc", h % 2)
            pqa, pqb = self.bank(), self.bank()
            for j in range(2):
                self.mm(self.ps[pqa][0:96, :], self.wuq[:, j, h * 96:(h + 1) * 96], self.cqn[j][:], j == 0, j == 1,
                        ["wuq", ("cqn", j)], [("ps", pqa)])
            for j in range(2):
                self.mm(self.ps[pqb][0:96, :], self.wuq[:, j, 384 + h * 96:384 + (h + 1) * 96], self.cqn[j][:],
                        j == 0, j == 1, ["wuq", ("cqn", j)], [("ps", pqb)])
            self.copy("act", qc[0:64, :], self.ps[pqa][0:64, :], [("ps", pqa)], [(kqc, 0)])
            r1, kr1 = self.t32()
            r2, kr2 = self.t32()
            self.tt("dve", r1[R, :], self.ps[pqa][R, :], self.ropet[R, 0, :], ALU.mult, [("ps", pqa), ("ropet", 0)], [kr1])
            self.tt("dve", r2[R, :], self.ps[pqb][R, :], self.ropet[R, 1, :], ALU.mult, [("ps", pqb), ("ropet", 1)], [kr2])
            self.tt("dve", qc[R, :], r1[R, :], r2[R, :], ALU.add, [kr1, kr2], [(kqc, 1)])
            pacc = self.bank((6, 7))
            LAG = 2
            pend = []
            for blk in range(nblk + LAG):
                if blk < nblk:
                    pss = self.bank(self.SCP)
                    kti = blk // 4
                    self.mm(self.ps[pss][:, :], self.kc[0:96, h, blk * 128:(blk + 1) * 128], qc[0:96, :], True, True,
                            [("kc", kti, h), ("kcr", kti, h), (kqc, 0), (kqc, 1), "kvrd"], [("ps", pss)])
                    P, kP = self.t16()
                    self.act(P[:, 0:512], self.ps[pss][:, :], AF.Exp, [("ps", pss)], [kP], scale=scale)
                    pend.append((blk, P, kP))
                yield
                if blk >= LAG:
                    b2, P, kP = pend.pop(0)
                    kti = b2 // 4
                    self.mm(self.ps[pacc][0:65, :], self.vaug[:, b2, h, 0:65], P[:, 0:512], b2 == 0, b2 == nblk - 1,
                            [("vaug", kti), ("vaug1", kti), kP, "kvrd"], [("ps", pacc)])
            osb, kosb = self.t32()
            self.copy("dve", osb[0:65, :], self.ps[pacc][0:65, :], [("ps", pacc)], [kosb])
            pden = self.bank()
            self.mm(self.ps[pden][0:64, :], cs[64:65, C_ONES:C_ONES + 64], osb[64:65, :], True, True,
                    ["consts", kosb], [("ps", pden)])
            rd, krd = self.t32()
            self.act(rd[0:64, :], self.ps[pden][0:64, :], AF.Ln, [("ps", pden)], [krd])
            self.act(rd[0:64, :], rd[0:64, :], AF.Exp, [krd], [krd], scale=-1.0)
            self.tt("dve", self.yh[h], osb[0:64, :], rd[0:64, :], ALU.mult, [kosb, krd], [("win16", h)])
        yield
        for j in range(2):
            pyc = self.bank()
            self.mm(self.ps[pyc][:, :], self.sel[:, 0:128], self.yh[2 * j], True, False, ["sel", ("win16", 2 * j)], [("ps", pyc)])
            self.mm(self.ps[pyc][:, :], self.sel[:, 128:256], self.yh[2 * j + 1], False, True,
                    ["sel", ("win16", 2 * j + 1)], [("ps", pyc)])
            self.copy("act", self.yg[j][:], self.ps[pyc][:, :], [("ps", pyc)], [("yg", j)])
        self.group_rms(l, 2)

    def b1_loads(self, l, ti):
        sc = self.sc
        t0 = ti * TW
        tcols = slice(t0, t0 + TW)
        self.pref.add(("b1", l, ti))
        self.load_hT(ti, ti % 2)
        self.load_rope(ti)
        nb = lambda key, j: [(key, ti, j), (key, min(ti + 1, self.NT - 1), j), (key, max(ti - 1, 0), j), key + "_pad"]
        for j in range(2):
            sc.dma("sp", self.win16[j][:, 0:TW + 2], self.scch_d[j * 128:(j + 1) * 128, t0:t0 + TW + 2],
                   reads=nb("scch", j), writes=[("win16", j)], key=("win16", j))
            sc.dma("sp", self.win16[2 + j][:, 0:TW + 30], self.glu_d[j * 128:(j + 1) * 128, t0:t0 + TW + 30],
                   reads=nb("glu", j), writes=[("win16", 2 + j)], key=("win16", 2 + j))
            sc.dma("sp", self.oDt[:, j, :], self.oD_d[j * 128:(j + 1) * 128, tcols], reads=[("oD", ti, j)],
                   writes=[("oDt", j)], key=("oDld", j))
        self.load_w(self.w_in_s, l, G1, 544, 0)

    def sweepB1_tile(self, l, ti):
        sc, cs, S = self.sc, self.consts, self.S
        vec = self.vecs
        t0 = ti * TW
        tcols = slice(t0, t0 + TW)
        hs = ti % 2
        if ("b1", l, ti) not in self.pref:
            self.b1_loads(l, ti)
        pga = self.bank()
        self.proj_fm(pga, 32, 0, 512, hs)
        self.copy("act", self.ga_sb[0:32, :], self.ps[pga][0:32, :], [("ps", pga)], ["ga_sb"])
        pq, pk2 = self.bank(), self.bank()
        self.proj_fm(pq, 128, 0, 0, hs)
        self.proj_fm(pk2, 128, 0, 128, hs)
        self.copy("act", self.q_sb[:], self.ps[pq][:, :], [("ps", pq)], ["q_sb"])
        self.copy("dve", self.k_sb[:], self.ps[pk2][:, :], [("ps", pk2)], ["k_sb"])
        self.load_w(self.w_in_s, l, B1G, 512, 2)
        for j in range(2):
            w, kw = self.win16[j], ("win16", j)
            acc, kacc = self.t32()
            wcol = lambda k: vec[:, l, V_SCW + k * 2 + j:V_SCW + k * 2 + j + 1]
            self.ts("dve", acc[:], w[:, 0:TW], wcol(0), ALU.mult, [kw, "vecs"], [kacc])
            self.stt(acc[:], w[:, 1:TW + 1], wcol(1), acc[:], ALU.mult, ALU.add, [kw, kacc, "vecs"], [kacc])
            self.stt(acc[:], w[:, 2:TW + 2], wcol(2), acc[:], ALU.mult, ALU.add, [kw, kacc, "vecs"], [kacc])
            pb_ = self.bank()
            self.proj_fm(pb_, 128, 2, j * 128, hs)
            self.tt("dve", self.yg[j][:], acc[:], self.ps[pb_][:, :], ALU.mult, [kacc, ("ps", pb_)], [("yg", j)])
        self.group_rms(l, 0)
        ud = self.long32[2:4]
        pm_, pq_ = self.bank((6, 7)), self.bank((6, 7))
        for j in range(2):
            w, kw = self.win16[2 + j], ("win16", 2 + j)
            pcf = self.bank()
            for k in range(31):
                dg, kdg = self.diag_tile(l, V_CFW + k * 2 + j)
                self.mm(self.ps[pcf][:, :], dg[:], w[:, k:k + TW], k == 0, k == 30, [kdg, kw], [("ps", pcf)])
            self.copy("dve", ud[j][:], self.ps[pcf][:, :], [("ps", pcf)], [("long32", 2 + j)])
            ub, kub = self.t16()
            uq, kuq = self.t16()
            self.copy("act", ub[:, 0:512], self.ps[pcf][:, :], [("ps", pcf)], [kub])
            self.act(uq[:, 0:512], self.ps[pcf][:, :], AF.Square, [("ps", pcf)], [kuq])
            self.mm(self.ps[pm_][:, :], self.ones256[:], ub[:, 0:512], j == 0, j == 1, ["ones256", kub], [("ps", pm_)])
            self.mm(self.ps[pq_][:, :], self.ones256[:], uq[:, 0:512], j == 0, j == 1, ["ones256", kuq], [("ps", pq_)])
        mean, var = self.mean_t, self.var_t
        msq, kmsq = self.t32()
        self.copy("dve", mean[:], self.ps[pm_][:, :], [("ps", pm_)], ["mean_t"])
        self.act(msq[:], self.ps[pm_][:, :], AF.Square, [("ps", pm_)], [kmsq])
        self.tt("dve", var[:], self.ps[pq_][:, :], msq[:], ALU.subtract, [("ps", pq_), kmsq], ["var_t"])
        self.rstd(var[:], var[:], ["var_t"], "var_t")
        for j in range(2):
            u, ku = ud[j], ("long32", 2 + j)
            self.tt("dve", u[:], u[:], mean[:], ALU.subtract, [ku, "mean_t"], [ku])
            self.stt(u[:], u[:], vec[:, l, V_CFG + j:V_CFG + j + 1], var[:], ALU.mult, ALU.mult, [ku, "var_t", "vecs"], [ku])
            self.act(self.yg[j][:], u[:], AF.Silu, [ku, "vecs"], [("yg", j)],
                     bias=vec[:, l, V_CFB + j:V_CFB + j + 1], scale=1.0)
        self.group_rms(l, 3)
        og = self.long32[0:2]

        def side():
            for sub in range(4):
                yield from self.gla_chunk(0, sub, True, hs, 0)
                po = self._po
                for j in range(2):
                    self.tt("dve", og[j][:, sub * 128:(sub + 1) * 128], self.ps[po][:, j * 128:(j + 1) * 128],
                            self.oDt[:, j, sub * 128:(sub + 1) * 128], ALU.add, [("ps", po), ("oDt", j)], [("long32", j)])

        self.merge(self.mla_gen(l, ti, hs), side())
        for j in range(2):
            sq, ksq = self.t16()
            self.act(sq[:, 0:512], og[j][:], AF.Square, [("long32", j)], [ksq])
            pms = self.bank()
            self.mm(self.ps[pms][:, :], self.bo64[:], sq[:, 0:512], True, True, ["bo64", ksq], [("ps", pms)])
            rs, krs = self.t32()
            self.rstd(rs[:], self.ps[pms][:, :], [("ps", pms)], krs)
            pgg = self.bank()
            self.proj_fm(pgg, 128, 2, 256 + j * 128, hs)
            sg, ksg = self.t32()
            self.act(sg[:], self.ps[pgg][:, :], AF.Silu, [("ps", pgg)], [ksg])
            self.stt(rs[:], og[j][:], vec[:, l, V_GH + j:V_GH + j + 1], rs[:], ALU.mult, ALU.mult,
                     [("long32", j), krs, "vecs"], [krs])
            self.tt("dve", self.yg[j][:], rs[:], sg[:], ALU.mult, [krs, ksg], [("yg", j)])
        self.group_rms(l, 1)
        sc.dma("sp", self.h32[:, :, :], self.hin32[:, tcols].rearrange("(c p) t -> p c t", p=128),
               reads=[("hin32", ti, bb, c) for bb in range(4) for c in range(8)], writes=[("h32", c) for c in range(8)],
               key="h32ld")
        for half in range(2):
            self.load_w(self.w_out_s, l, half * 512, 512, 1 + half)
        if ti + 1 < self.NTO:
            self.b1_loads(l, ti + 1)
        elif self.NTO >= 3:
            self.b2_loads(l, 0)
            self.w_up_load(l, 0, 0)
        for half in range(2):
            for mm_ in range(4):
                m = half * 4 + mm_
                pmx = self.bank()
                for kc in range(8):
                    self.mm(self.ps[pmx][:, :], self.wbuf[1 + half][:, kc, mm_ * 128:(mm_ + 1) * 128], self.ybuf[:, kc, :],
                            kc == 0, kc == 7, [("wbuf", 1 + half), ("ybuf", kc)], [("ps", pmx)])
                self.stt(self.big32[:, m, :], self.h32[:, m, :], ALPHA, self.ps[pmx][:, :], ALU.mult, ALU.add,
                         [("h32", m), ("ps", pmx)], [("big32", m)])
        self.layernorm8(l, V_L1G, V_L1B)
        for m in range(8):
            self.copy("dve", self.ybuf[:, m, :], self.h32[:, m, :], [("h32", m)], [("ybuf", m)])
        sc.dma(STQ, self.h1_16[:, 1 + t0:1 + t0 + TW].rearrange("(c p) t -> p c t", p=128), self.ybuf[:, 0:8, :],
               reads=[("ybuf", m) for m in range(8)], writes=[("h1_16", ti, m) for m in range(8)], key="h1_16st")
        sc.dma(STQ, self.h1_32[:, tcols].rearrange("(c p) t -> p c t", p=128), self.h32[:, :, :],
               reads=[("h32", m) for m in range(8)], writes=[("h1_32", ti, m) for m in range(8)], key="h1_32st")

    def b2_loads(self, l, tj):
        sc = self.sc
        self.pref.add(("b2", l, tj))
        h1deps = []
        for kc in range(8):
            h1deps += [("h1_16", tj, kc), ("h1_16", max(tj - 1, 0), kc), ("h1_16", min(tj + 1, self.NTO - 1), kc)]
        if tj == 0:
            h1deps.append("h1padL")
        if tj == self.NTO - 1:
            h1deps.append("h1padR")
        sc.dma("sp", self.hT[tj % 2][:, :, :], self.h1_16[:, tj * TW:tj * TW + TW + 2].rearrange("(c p) t -> p c t", p=128),
               reads=h1deps, writes=[("hT", tj % 2, kc) for kc in range(8)], key=("hT", tj % 2))

    def w_up_load(self, l, tj, m):
        sc = self.sc
        slot = m % 3
        self.pref.add(("b2w", l, tj, m))
        sc.dma("sp", self.wbuf[slot][:, :, 0:128], self.w_up_s[l, :, m, :, :],
               reads=self.wk[l]["up"], writes=[("wbufB", slot, 0), ("wbuf", slot)], key=("wbuf", slot, 0))
        sc.dma("sp", self.wbuf[slot][:, :, 128:256], self.w_up_s[l, :, 22 + m, :, :],
               reads=self.wk[l]["up"], writes=[("wbufB", slot, 1), ("wbuf", slot)], key=("wbuf", slot, 1))

    def b2_S1(self, l, ti, i, pool):
        sc = self.sc
        hs = ti % 2
        st = self.b2st.setdefault((l, ti), {"ust": {}, "pcs": {}, "pre": 0})
        m, which = i // 2, i % 2
        slot = m % 3
        halo = lambda kc: self.hT[hs][:, kc, 0:TW + 2:TW + 1]
        if which == 0 and ("b2w", l, ti, m) not in self.pref:
            self.w_up_load(l, ti, m)
        pu, puh = self.bank(pool), self.bank(pool)
        for kc in range(8):
            self.mm(self.ps[pu][:, :], self.wbuf[slot][:, kc, which * 128:(which + 1) * 128],
                    self.hT[hs][:, kc, 1:TW + 1], kc == 0, kc == 7, [("wbufB", slot, which), ("wbuf", slot), ("hT", hs, kc)], [("ps", pu)])
        for kc in range(8):
            self.mm(self.ps[puh][:, 0:2], self.wbuf[slot][:, kc, which * 128:(which + 1) * 128],
                    halo(kc), kc == 0, kc == 7, [("wbufB", slot, which), ("wbuf", slot), ("hT", hs, kc)], [("ps", puh)])
        u, ku = self.t16()
        self.copy("act", u[:, 1:TW + 1], self.ps[pu][:, :], [("ps", pu)], [(ku, 0)])
        self.copy("dve", u[:, 0:TW + 2:TW + 1], self.ps[puh][:, 0:2], [("ps", puh)], [(ku, 1)])
        st["ust"][i] = (u, ku)

    def sweepB2_tile(self, l, ti, last):
        sc, nc, cs = self.sc, self.nc, self.consts
        t0 = ti * TW
        tcols = slice(t0, t0 + TW)
        hs = ti % 2
        if ("b2", l, ti) not in self.pref:
            self.b2_loads(l, ti)
        halo = lambda kc: self.hT[hs][:, kc, 0:TW + 2:TW + 1]
        P8 = (0, 1, 2, 3, 4, 5, 6, 7)
        items = [(m, w) for m in range(22) for w in range(2)]
        st = self.b2st.setdefault((l, ti), {"ust": {}, "pcs": {}, "pre": 0})
        ust, pcs = st["ust"], st["pcs"]
        PRE = 3

        def S2(i):
            m, which = items[i]
            chunk = m + 22 * which
            u, ku = ust.pop(i)
            wc = lambda k: self.vecs[:, l, V_FDW + k * 44 + chunk:V_FDW + k * 44 + chunk + 1]
            t, kt = self.t32()
            if CONV_ON == "pe":
                pc_ = self.bank(P8)
                for k in range(3):
                    dg, kdg = self.diag_tile(l, V_FDW + k * 44 + chunk)
                    self.mm(self.ps[pc_][:, :], dg[:], u[:, k:k + TW], k == 0, k == 2,
                            [kdg, (ku, 0), (ku, 1)], [("ps", pc_)])
                pcs[i] = (self.ps[pc_][:, :], ("ps", pc_))
            else:
                self.act(t[:], u[:, 0:TW], AF.Identity, [ku, "vecs"], [kt], scale=wc(0))
                self.stt(t[:], u[:, 1:TW + 1], wc(1), t[:], ALU.mult, ALU.add, [ku, kt, "vecs"], [kt])
                self.stt(t[:], u[:, 2:TW + 2], wc(2), t[:], ALU.mult, ALU.add, [ku, kt, "vecs"], [kt])
                pcs[i] = (t[:], kt)

        def S3(m):
            (a0, k0), (a1, k1) = pcs.pop(2 * m), pcs.pop(2 * m + 1)
            if CONV_ON == "pe":
                sg, ksg = self.t32()
            else:
                sg, ksg = a0, k0
            self.act(sg if CONV_ON != "pe" else sg[:], a0, AF.Silu, [k0], [ksg])
            self.tt("dve", self.actb[:, m, :], sg if CONV_ON != "pe" else sg[:], a1, ALU.mult, [ksg, k1, "kvtok"], [("actb", m)])

        def w_dn_load(mo):
            sc.dma("sp", self.wdn[mo % 2], self.w_dn_s[l, :, mo, :, :], reads=self.wk[l]["dn"] + ["kvtok"],
                   writes=[("wdn", mo % 2)], key=("wdn", mo % 2))

        if ti > 0:
            w_dn_load(0)
            w_dn_load(1)
        for i in range(len(items) + 1):
            if ti == 0 and i == 6:
                w_dn_load(0)
                w_dn_load(1)
            if i < len(items) and i >= st["pre"]:
                self.b2_S1(l, ti, i, P8)
            if i >= 1:
                S2(i - 1)
                if (i - 1) % 2 == 1:
                    S3((i - 1) // 2)
        sc.dma("sp", self.h32[:, :, :], self.h1_32[:, tcols].rearrange("(c p) t -> p c t", p=128),
               reads=[("h1_32", ti, c) for c in range(8)], writes=[("h32", c) for c in range(8)], key="h32ld")
        if ti + 1 < self.NTO:
            if self.pair and ti + 1 == self.NTO - 1:
                self.exchange_h1_recv(l)
            self.b2_loads(l, ti + 1)
            for m_ in range(3):
                self.w_up_load(l, ti + 1, m_)
        for mo in range(8):
            slot = mo % 2
            if mo >= 2:
                w_dn_load(mo)
            pd = self.bank()
            for m in range(22):
                self.mm(self.ps[pd][:, :], self.wdn[slot][:, m, :], self.actb[:, m, :], m == 0, m == 21,
                        [("wdn", slot), ("actb", m)], [("ps", pd)])
            self.stt(self.big32[:, mo, :], self.h32[:, mo, :], ALPHA, self.ps[pd][:, :], ALU.mult, ALU.add,
                     [("h32", mo), ("ps", pd)], [("big32", mo)])
        def early():
            if ti + 1 < self.NTO and _os.environ.get("B2_EARLY", "1") == "1":
                for i_ in range(PRE):
                    self.b2_S1(l, ti + 1, i_, (0, 1, 2, 3, 4, 5))
                self.b2st[(l, ti + 1)]["pre"] = PRE

        self.layernorm8(l, V_L2G, V_L2B, mid=early)
        if last or self.pair:
            flat = self.big32[:].rearrange("p a b -> p (a b)")
            yflat = self.ybuf[:].rearrange("p a b -> p (a b)")
            for b in range(4):
                ot = flat[:, b * 1024:(b + 1) * 1024] if last else yflat[:, b * 1024:(b + 1) * 1024]
                okey = "big32" if last else "ybuf"
                for g in range(2):
                    pb = self.bank()
                    for j in range(4):
                        c = g * 4 + j
                        sc.op("pe", lambda c=c, j=j, pb=pb, b=b: nc.tensor.transpose(
                            out=self.ps[pb][:, j * 128:(j + 1) * 128], in_=self.h32[:, c, b * 128:(b + 1) * 128],
                            identity=cs[:, C_ID:C_ID + 128]), reads=[("h32", c), "consts"], writes=[("ps", pb)])
                    self.copy("act" if g == 0 else "dve", ot[:, g * 512:(g + 1) * 512], self.ps[pb][:, :],
                              [("ps", pb)], [(okey, 2 * b + g)])
                dst = self.out[t0 + b * 128:t0 + (b + 1) * 128, :] if last else self.xch_in[ti][b * 128:(b + 1) * 128, :]
                sc.dma(STQ, dst, ot,
                       reads=[(okey, 2 * b), (okey, 2 * b + 1)], writes=[("out" if last else "xch_in", ti, b)], key="outst")
            if not last:
                RG = self.RG
                sc.op("pool", lambda: nc.gpsimd.collective_compute("AllGather", ALU.bypass, replica_groups=RG,
                                                                   ins=[self.xch_in[ti].ap()], outs=[self.xch_out[ti].ap()]),
                      reads=[("xch_in", ti, b) for b in range(4)], writes=[("xch_out", ti)], dma_key=("cc_e2", ti), cc=True)
        if not last:
            st16 = self.big32[:].rearrange("p a b -> p (a b)")[:, 0:2048].bitcast(BF16).rearrange("p (c t) -> p c t", c=8)
            for m in range(8):
                self.copy("dve", st16[:, m, :], self.h32[:, m, :], [("h32", m)], [("big32", m // 2)])
            sc.dma(STQ, self.hin16[:, tcols].rearrange("(c p) t -> p c t", p=128), st16,
                   reads=[("big32", c) for c in range(4)],
                   writes=[("hin16", ti, bb, m) for bb in range(4) for m in range(8)], key="hin16st")
            sc.dma(STQ, self.hin32[:, tcols].rearrange("(c p) t -> p c t", p=128), self.h32[:, :, :],
                   reads=[("h32", m) for m in range(8)],
                   writes=[("hin32", ti, bb, m) for bb in range(4) for m in range(8)], key="hin32st")

    def exchange_h1_halo(self, l):
        sc, nc, cs, T = self.sc, self.nc, self.consts, self.T
        sm = self.small
        self.copy("dve", sm[:, 0:8], self.h32[:, :, TW - 1], [("h32", m) for m in range(8)], ["small"])
        sc.dma(STQ, self.e1_in.ap(), sm[:, 0:8], reads=["small"], writes=["e1_in"], key="e1st")
        RG = self.RG
        sc.op("pool", lambda: nc.gpsimd.collective_compute("AllGather", ALU.bypass, replica_groups=RG,
                                                           ins=[self.e1_in.ap()], outs=[self.e1_out.ap()]),
              reads=["e1_in"], writes=["e1_out"], dma_key="cc_e1", cc=True)

    def exchange_h1_recv(self, l):
        sc, nc, cs, T = self.sc, self.nc, self.consts, self.T
        sm = self.small
        sc.dma("sp", sm[:, 0:8], self.e1_out[0:128, :], reads=["e1_out"], writes=["small"], key="e1ld")
        sc.dma("sp", sm[:, 8:16], self.e1_out[128:256, :], reads=["e1_out"], writes=["small2"], key="e1ld")
        t, kt = self.t32()
        self.ts("dve", t[:, 0:8], sm[:, 0:8], cs[:, C_SELW:C_SELW + 1], ALU.mult, ["small", "consts"], [kt])
        o16, ko = self.t16()
        self.stt(o16[:, 0:8], sm[:, 8:16], cs[:, C_SELW + 1:C_SELW + 2], t[:, 0:8], ALU.mult, ALU.add,
                 ["small2", kt, "consts"], [ko])
        dst = self.h1_16[:, T + 1:T + 2].rearrange("(c p) o -> p (c o)", p=128)
        sc.dma(STQ, dst, o16[:, 0:8], reads=[ko], writes=["h1padR"], key=("sto", ko), slow=True)

    def exchange_stream(self, l):
        sc, nc, T = self.sc, self.nc, self.T
        yflat = self.ybuf[:].rearrange("p a b -> p (a b)")
        b16 = self.big32[:].rearrange("p a b -> p (a b)")[:, 0:1536].bitcast(BF16)
        sets = [(yflat[:, 0:2048], [[("ybuf", 0), ("ybuf", 1)], [("ybuf", 2), ("ybuf", 3)]],
                 self.ybuf[:, 4:6, :], [("ybuf", 4), ("ybuf", 5)]),
                (b16[:, 0:2048], [[("big32", 0)], [("big32", 1)]],
                 b16[:, 2048:3072].rearrange("p (g t) -> p g t", g=2), [("big32", 2), ("big32", 2)])]
        order = sorted(range(T // 128), key=lambda b_: ((T - 128 * (b_ + 1)) // TW, b_))
        for n_, bp in enumerate(order):
            ldv, ldk, outv, outk = sets[n_ % 2]
            r0 = T - 128 * (bp + 1)
            tj, rr = r0 // TW, r0 % TW
            for s_ in range(2):
                sc.dma("sp", ldv[:, s_ * 1024:(s_ + 1) * 1024], self.xch_out[tj][s_ * TW + rr:s_ * TW + rr + 128, :],
                       reads=[("xch_out", tj)], writes=ldk[s_], key=("x2ld", n_ % 2, s_))
            for g in range(2):
                pb = self.bank()
                for j in range(4):
                    c = g * 4 + j
                    for s_ in range(2):
                        self.mm(self.ps[pb][:, j * 128:(j + 1) * 128], ldv[:, s_ * 1024 + c * 128:s_ * 1024 + (c + 1) * 128],
                                self.Jsel[:, s_, :], s_ == 0, s_ == 1, ldk[s_] + ["Jsel"], [("ps", pb)])
                self.copy("act" if g == 0 else "dve", outv[:, g, :], self.ps[pb][:, :], [("ps", pb)], [outk[g]])
            col0 = T + bp * 128
            ti_, bb = col0 // TW, (col0 % TW) // 128
            for c in range(8):
                sc.dma(STQ, self.hin16[c * 128:(c + 1) * 128, col0:col0 + 128],
                       outv[:, c // 4, (c % 4) * 128:(c % 4 + 1) * 128],
                       reads=[outk[c // 4]], writes=[("hin16", ti_, bb, c)], key=("x2st", n_ % 2, c % 2))

    def build(self, n_layers=None, stop=None):
        L = self.L if n_layers is None else n_layers
        self.setup()
        LATE = L > 1 and _os.environ.get("LATE_CAST", "1") == "1"

        def casts():
            for l_ in range(1 if LATE else L):
                yield from self.cast_weights(l_)

        late_gen = self.cast_weights(1, late=True) if LATE else None

        def prep():
            yield from self.phase0()
            yield from self.rope_tables()

        self.merge(casts(), prep())
        for l in range(L):
            if stop in ("SETUP", "P0", "P0a"):
                break
            self.sc.fence()
            self.layer_weights_small(l)
            self.memset("dve", self.Sst[:], 0.0, ["Sst"])
            self.memset("pool", self.Sbf[:], 0.0, ["Sbf"])
            for ti in reversed(range(self.NT)):
                self.sweepA_tile(l, ti, late_gen if l == 0 else None)
            if l == 0 and late_gen is not None:
                for _ in late_gen:
                    pass
            if stop == "A":
                break
            self.memset("dve", self.Sst[:], 0.0, ["Sst"])
            self.memset("pool", self.Sbf[:], 0.0, ["Sbf"])
            for ti in range(self.NTO):
                self.sweepB1_tile(l, ti)
            if stop == "B1":
                break
            self.memset("dve", self.tok[:], 0.0, ["kvtok"])
            self.sc.ops[-1]["deps"].update({j: "war" for j in self.sc.readers.get("kvrd", ())})
            if self.pair:
                self.exchange_h1_halo(l)
                if self.NTO == 1:
                    self.exchange_h1_recv(l)
            for ti in range(self.NTO):
                self.sweepB2_tile(l, ti, last=(l == L - 1))
            if self.pair and l < L - 1:
                self.exchange_stream(l)
        st = self.sc.emit()
        return self.nc, st


def kchunk(w):
    K, N = w.shape
    return np.ascontiguousarray(w.reshape(K // 128, 128, N).transpose(1, 0, 2))


def pvec(v):
    return np.ascontiguousarray(np.asarray(v).reshape(-1, 128).T)


def make_consts(rev, ln_in_g, ln_in_b, slot):
    c = np.zeros((128, NCONST), np.float32)
    m = np.zeros((128, NMASK), np.float32)
    c[:, C_ID:C_ID + 128] = np.eye(128, dtype=np.float32)
    s = np.arange(128)[:, None]
    t = np.arange(128)[None, :]
    c[:, C_TRIU:C_TRIU + 128] = np.where(s <= t, -1.0 / 16.0, 0.0)
    c[:, C_TRID:C_TRID + 128] = np.where(s >= t, -1.0 / 16.0, 0.0)
    c[:, C_ONES:C_ONES + 128] = 1.0
    if not rev:
        m[:, M_MU:M_MU + 128] = (s <= t)
        m[:, M_MD:M_MD + 128] = (s > t)
    else:
        m[:, M_MU:M_MU + 128] = (s < t)
        m[:, M_MD:M_MD + 128] = (s >= t)
    hk = np.arange(128)[:, None] // 32
    hv = np.arange(256)[None, :] // 64
    m[:, M_BM:M_BM + 256] = (hk == hv)
    m[:, M_CME:M_CME + 256] = (hv % 2 == 0)
    m[:, M_CMO:M_CMO + 256] = (hv % 2 == 1)
    for i in range(128):
        m[i, M_J + 127 - i] = 1.0
    for i in range(64):
        m[i, M_SEL + i] = 1.0
        m[i, M_SEL + 128 + 64 + i] = 1.0
    for h in range(4):
        c[:, C_HM + h] = (np.arange(128) // 32 == h)
    inv = (10000.0 ** (-np.arange(0, 32, 2, dtype=np.float32) / 32)).astype(np.float32)
    c[64:80, C_ROPE] = inv
    c[80:96, C_ROPE] = inv
    c[64:80, C_ROPE + 1] = -1.0
    c[80:96, C_ROPE + 1] = 1.0
    c[:, C_LNIN:C_LNIN + 8] = pvec(ln_in_g)
    c[:, C_LNIN + 8:C_LNIN + 16] = pvec(ln_in_b)
    c[:, C_SELW + slot] = 1.0
    return c, m


def prep_layer_weights(inp, rev):
    L = inp["w_in"].shape[0]
    w_in_e, w_uq_e, w_ukv_e, w_a_e, vecs = [], [], [], [], []
    for l in range(L):
        w = np.asarray(inp["w_in"][l])
        sc_b, sc_c, sc_h = w[:, 0:256], w[:, 256:512], w[:, 512:768]
        gq, gk, gv, gg = w[:, 768:896], w[:, 896:1024], w[:, 1024:1280], w[:, 1280:1536]
        ga_f, ga_b = w[:, 1536:1552], w[:, 1552:1568]
        cq, ckv, kr = w[:, 1568:1824], w[:, 1824:1952], w[:, 1952:1984]
        ca, cg = w[:, 1984:2240], w[:, 2240:2496]
        z64 = np.zeros((1024, 64), np.float32)
        kr_sw = np.concatenate([kr[:, 16:32], kr[:, 0:16]], axis=1)
        ga_u, ga_d = (ga_f, ga_b) if not rev else (ga_b, ga_f)
        ext = np.concatenate([gq, gk, gv, ga_u, ga_d, ckv, z64, kr, z64, kr_sw, sc_c, sc_h, ca, cg, sc_b, gg, cq], axis=1)
        assert ext.shape[1] == NEXT
        w_in_e.append(kchunk(ext))
        wq = np.asarray(inp["w_mla_uq"][l])
        norm, swp = [], []
        for h in range(4):
            blk = wq[:, h * 96:(h + 1) * 96]
            norm.append(blk)
            swp.append(np.concatenate([blk[:, 0:64], blk[:, 80:96], blk[:, 64:80]], axis=1))
        w_uq_e.append(kchunk(np.concatenate(norm + swp, axis=1)))
        wkv = np.asarray(inp["w_mla_ukv"][l])
        kp = [wkv[:, h * 128:h * 128 + 64] for h in range(4)]
        vp = [wkv[:, h * 128 + 64:h * 128 + 128] for h in range(4)]
        w_ukv_e.append(np.ascontiguousarray(np.concatenate(kp + vp, axis=1)))
        wa = np.asarray(inp["w_gla_a_up"][l])
        ba = np.asarray(inp["b_gla_a"][l])
        iu, idn = (0, 1) if not rev else (1, 0)
        wae = np.zeros((33, 256), np.float32)
        wae[0:16, 0:128] = wa[iu]
        wae[16:32, 128:256] = wa[idn]
        wae[32, 0:128] = ba[iu]
        wae[32, 128:256] = ba[idn]
        w_a_e.append(wae)
        v = np.zeros((128, NV), np.float32)
        v[:, V_GH:V_GH + 2] = pvec(inp["g_gla_head"][l])
        v[:, V_GQ:V_GQ + 2] = pvec(inp["g_mla_q"][l])
        v[:, V_GKV:V_GKV + 1] = pvec(inp["g_mla_kv"][l])
        scw = np.asarray(inp["w_sc_conv"][l])
        cfw = np.asarray(inp["w_cfm_dw"][l])
        fdw = np.asarray(inp["w_ffn_dw"][l])
        if rev:
            scw, cfw, fdw = scw[::-1], cfw[::-1], fdw[::-1]
        for k in range(3):
            v[:, V_SCW + 2 * k:V_SCW + 2 * k + 2] = pvec(scw[k])
            v[:, V_FDW + 44 * k:V_FDW + 44 * k + 44] = pvec(fdw[k])
        for k in range(31):
            v[:, V_CFW + 2 * k:V_CFW + 2 * k + 2] = pvec(cfw[k])
        v[:, V_CFG:V_CFG + 2] = pvec(inp["g_cfm_ln"][l])
        v[:, V_CFB:V_CFB + 2] = pvec(inp["b_cfm_ln"][l])
        v[:, V_GBR:V_GBR + 8] = pvec(inp["g_branch"][l])
        v[:, V_L1G:V_L1G + 8] = pvec(inp["ln1_g"][l])
        v[:, V_L1B:V_L1B + 8] = pvec(inp["ln1_b"][l])
        v[:, V_L2G:V_L2G + 8] = pvec(inp["ln2_g"][l])
        v[:, V_L2B:V_L2B + 8] = pvec(inp["ln2_b"][l])
        vecs.append(v)
    return dict(w_in=np.stack(w_in_e), w_uq=np.stack(w_uq_e), w_ukv=np.stack(w_ukv_e), w_a=np.stack(w_a_e),
                vecs=np.stack(vecs),
                w_out=np.stack([kchunk(np.asarray(inp["w_out"][l])) for l in range(L)]),
                w_up=np.stack([kchunk(np.asarray(inp["w_ffn_up"][l])) for l in range(L)]),
                w_dn=np.stack([kchunk(np.asarray(inp["w_ffn_down"][l])) for l in range(L)]))


PAIR = True


def build_inputs(inputs, Bn, S, pair):
    x = np.asarray(inputs["x"], np.float32)
    pos = np.asarray(inputs["positions"], np.int32)
    in_maps = []
    if not pair:
        wf = prep_layer_weights(inputs, False)
        cst, msk = make_consts(False, inputs["ln_in_g"], inputs["ln_in_b"], 0)
        for c in range(8):
            b = c % Bn
            m = dict(wf)
            m.update(x=np.ascontiguousarray(x[b]), pos=np.ascontiguousarray(pos[b][None, :]), consts=cst, masks=msk)
            in_maps.append(m)
        return in_maps
    wfs = [prep_layer_weights(inputs, False), prep_layer_weights(inputs, True)]
    for c in range(2 * Bn):
        b, r = c // 2, c % 2
        m = dict(wfs[r])
        xs, ps = (x[b], pos[b]) if r == 0 else (x[b][::-1], pos[b][::-1])
        cst, msk = make_consts(r == 1, inputs["ln_in_g"], inputs["ln_in_b"], 1 - r)
        m.update(x=np.ascontiguousarray(xs), pos=np.ascontiguousarray(ps[None, :]), consts=cst, masks=msk)
        in_maps.append(m)
    return in_maps


def assemble(results, Bn, S, pair):
    if not pair:
        return np.stack([np.asarray(results[b]["out"]) for b in range(Bn)], axis=0).astype(np.float32)
    T = S // 2
    out = np.zeros((Bn, S, D), np.float32)
    for b in range(Bn):
        out[b, 0:T] = np.asarray(results[2 * b]["out"])
        out[b, T:S] = np.asarray(results[2 * b + 1]["out"])[::-1]
    return out


def kernel(**inputs):
    Bn, S, _ = inputs["x"].shape
    L = inputs["w_in"].shape[0]
    pair = PAIR and (2 * Bn <= 8)
    bld = Builder(S, S // 2 if pair else S, L, pair=pair)
    nc, st = bld.build()
    in_maps = build_inputs(inputs, Bn, S, pair)
    res = run_bass_kernel_spmd(nc, in_maps, core_ids=list(range(len(in_maps))))
    return assemble(res.results, Bn, S, pair)
```
